# Optimizing a Trainium2 kernel written in Bass

```python
import math
import jax, jax.numpy as jnp
from jax import lax
import numpy as np

D_MODEL = 1024
BATCH = 8
SEQ = 4096
DEPTH = 2

GRID_W = 64
CTX_LEN = 256
EPS = 1e-6
ROPE_BASE = 10000.0
MIX_WIDTH = D_MODEL
HALF_MIX = MIX_WIDTH // 2
SHORT_CONV_W = 5

A_HEAD_DIM = 64
A_HEADS = HALF_MIX // A_HEAD_DIM
A_KV_HEADS = A_HEADS // 4
A_GROUP = A_HEADS // A_KV_HEADS
A_WINDOW = 128
A_BLOCK = 128

B_HEADS = 4
B_V_DIM = HALF_MIX // B_HEADS
B_QK_DIM = B_V_DIM // 2
B_CHUNK = 64

C_HEAD_DIM = 64
C_HEADS = HALF_MIX // (2 * C_HEAD_DIM)
C_BLOCK = 128

D_HEAD_DIM = 64
D_HEADS = HALF_MIX // D_HEAD_DIM
D_GROUPS = 2
D_STATE = 128
D_CHUNK = 128

FFN_HIDDEN = ((8 * D_MODEL // 3 + 255) // 256) * 256
FFN_CONV_W = 3

A_Q = A_HEADS * A_HEAD_DIM
A_KV = A_KV_HEADS * A_HEAD_DIM
B_QK = 2 * B_HEADS * B_QK_DIM
B_V = B_HEADS * B_V_DIM
B_GATES = 4 * B_HEADS
C_QK = C_HEADS * 2 * C_HEAD_DIM
C_V = C_HEADS * 2 * C_HEAD_DIM
D_INNER = D_HEADS * D_HEAD_DIM
D_XBC = D_INNER + 2 * D_GROUPS * D_STATE
EVEN_SIZES = (A_Q, A_KV, A_KV, B_QK, B_V, B_V, B_GATES)
ODD_SIZES = (C_QK, C_QK, C_V, D_INNER, D_XBC, 2 * D_HEADS)
EVEN_IN = sum(EVEN_SIZES)
ODD_IN = sum(ODD_SIZES)

kernel_name = 'hybrid_diffusion_gqa_mlstm_diffattn_ssd'


def rmsnorm(x, g):
    xf = x.astype(jnp.float32)
    y = xf * lax.rsqrt(jnp.mean(xf * xf, axis=-1, keepdims=True) + EPS)
    return (y * g.astype(jnp.float32)).astype(x.dtype)


def modulate(u, shift, scale):
    return u * (1.0 + scale) + shift


def split_cols(y, sizes):
    return jnp.split(y, [int(s) for s in np.cumsum(sizes)[:-1]], axis=-1)


def dwconv(x, w):
    k = w.shape[0]
    pad = k // 2
    t = x.shape[1]
    xp = jnp.pad(x, ((0, 0), (pad, pad), (0, 0)))
    y = xp[:, 0:t] * w[0]
    for j in range(1, k):
        y = y + xp[:, j:j + t] * w[j]
    return y


def axial_rope_tables(rows, head_dim):
    pos = jnp.arange(rows * GRID_W)
    row = (pos // GRID_W).astype(jnp.float32)
    col = (pos % GRID_W).astype(jnp.float32)
    nf = head_dim // 4
    inv = ROPE_BASE ** (-jnp.arange(nf, dtype=jnp.float32) / nf)
    ar = row[:, None] * inv
    ac = col[:, None] * inv
    return (jnp.cos(ar), jnp.sin(ar), jnp.cos(ac), jnp.sin(ac))


def _rot_half(x, cos, sin):
    x1, x2 = jnp.split(x, 2, axis=-1)
    return jnp.concatenate([x1 * cos - x2 * sin, x2 * cos + x1 * sin], axis=-1)


def rope_2d(x, tabs):
    def shape(t):
        return t.reshape(t.shape[:1] + (1,) * (x.ndim - 3) + t.shape[1:]).astype(x.dtype)
    cr, sr, cc, sc = (shape(t) for t in tabs)
    xr, xc = jnp.split(x, 2, axis=-1)
    return jnp.concatenate([_rot_half(xr, cr, sr), _rot_half(xc, cc, sc)], axis=-1)


def _flip_time(arrays):
    return tuple(jnp.flip(a, axis=1) for a in arrays)


def bidir(run, ctx_dirs, lat_dirs, zero_state, need_ctx):
    y_ctx, y_lat = None, None
    for d in range(2):
        ci, li = ctx_dirs[d], lat_dirs[d]
        if d == 1:
            ci, li = _flip_time(ci), _flip_time(li)
        yc, state = run(*ci, zero_state, need_ctx)
        yl, _ = run(*li, state, True)
        if d == 1:
            yl = jnp.flip(yl, axis=1)
            yc = jnp.flip(yc, axis=1) if need_ctx else None
        y_lat = yl if y_lat is None else y_lat + yl
        if need_ctx:
            y_ctx = yc if y_ctx is None else y_ctx + yc
    return y_ctx, y_lat


def mlstm_chunked(q, k, v, log_i, log_f, state, want_out):
    bsz, t, nh, _ = q.shape
    L = B_CHUNK
    nc = t // L

    def ch(a):
        return a.reshape((bsz, nc, L) + a.shape[2:])
    q, k, v = ch(q), ch(k), ch(v)
    li = jnp.swapaxes(ch(log_i), 2, 3)
    b = jnp.cumsum(jnp.swapaxes(ch(log_f), 2, 3), axis=-1)
    b_last = b[..., -1]
    w = b_last[..., None] - b + li
    m_loc = jnp.max(w, axis=-1)
    e = jnp.exp(w - m_loc[..., None])
    c_loc = jnp.einsum('bchl,bclhv,bclhk->bchvk', e, v, k)
    n_loc = jnp.einsum('bchl,bclhk->bchk', e, k)

    def step(carry, inp):
        c_prev, n_prev, m_prev = carry
        bl, cl, nl, ml = inp
        m_new = jnp.maximum(bl + m_prev, ml)
        a = jnp.exp(bl + m_prev - m_new)
        g = jnp.exp(ml - m_new)
        c_new = a[..., None, None] * c_prev + g[..., None, None] * cl
        n_new = a[..., None] * n_prev + g[..., None] * nl
        return (c_new, n_new, m_new), (c_prev, n_prev, m_prev)

    lead = lambda a: jnp.moveaxis(a, 1, 0)
    final, (c0, n0, m0) = lax.scan(step, state, (lead(b_last), lead(c_loc), lead(n_loc), lead(m_loc)))
    if not want_out:
        return None, final
    c0, n0, m0 = jnp.moveaxis(c0, 0, 1), jnp.moveaxis(n0, 0, 1), jnp.moveaxis(m0, 0, 1)
    tri = jnp.tril(jnp.ones((L, L), dtype=bool))
    log_d = jnp.where(tri, b[..., :, None] - b[..., None, :] + li[..., None, :], -jnp.inf)
    g_in = b + m0[..., None]
    m_t = jnp.maximum(jnp.max(log_d, axis=-1), g_in)
    s = jnp.einsum('bcthk,bcshk->bchts', q, k) * jnp.exp(log_d - m_t[..., None])
    a_in = jnp.exp(g_in - m_t)
    num = jnp.einsum('bchts,bcshv->bcthv', s, v) + jnp.einsum('bcht,bchvk,bcthk->bcthv', a_in, c0, q)
    den = jnp.sum(s, axis=-1) + a_in * jnp.einsum('bchk,bcthk->bcht', n0, q)
    den = jnp.maximum(jnp.abs(den), jnp.exp(-m_t))
    h = num / jnp.swapaxes(den, 2, 3)[..., None]
    return h.reshape(bsz, t, nh, -1), final


def ssd_chunked(x, dt, a, bm, cm, state, want_out):
    bsz, t = x.shape[:2]
    L = D_CHUNK
    nc = t // L

    def ch(z):
        return z.reshape((bsz, nc, L) + z.shape[2:])
    x, dt, a, bm, cm = ch(x), ch(dt), ch(a), ch(bm), ch(cm)
    acum = jnp.cumsum(a, axis=2)
    a_last = acum[:, :, -1]
    wst = jnp.exp(a_last[:, :, None] - acum) * dt
    s_loc = jnp.einsum('bclgh,bclgn,bclghp->bcghpn', wst, bm, x)

    def step(h, inp):
        da, sl = inp
        return jnp.exp(da)[..., None, None] * h + sl, h

    final, h0 = lax.scan(step, state, (jnp.moveaxis(a_last, 1, 0), jnp.moveaxis(s_loc, 1, 0)))
    if not want_out:
        return None, final
    h0 = jnp.moveaxis(h0, 0, 1)
    ac = jnp.moveaxis(acum, 2, -1)
    tri = jnp.tril(jnp.ones((L, L), dtype=bool))
    seg = jnp.where(tri, ac[..., :, None] - ac[..., None, :], -jnp.inf)
    cb = jnp.einsum('bctgn,bcsgn->bcgts', cm, bm)
    mix = cb[:, :, :, None] * jnp.exp(seg) * jnp.moveaxis(dt, 2, -1)[..., None, :]
    y = jnp.einsum('bcghts,bcsghp->bctghp', mix, x)
    y = y + jnp.einsum('bctgn,bcghpn->bctghp', cm, h0) * jnp.exp(acum)[..., None]
    return y.reshape((bsz, t) + y.shape[3:]), final


def windowed_gqa_latent(q, k, v, k_ctx, v_ctx, sink):
    bsz, t = q.shape[:2]
    w = A_BLOCK
    nb = t // w
    scale = A_HEAD_DIM ** -0.5

    def windows(z):
        zp = jnp.pad(z, ((0, 0), (w, w), (0, 0), (0, 0))).reshape(bsz, nb + 2, w, A_KV_HEADS, A_HEAD_DIM)
        return jnp.concatenate([zp[:, :-2], zp[:, 1:-1], zp[:, 2:]], axis=2)
    kw, vw = windows(k), windows(v)
    qb = q.reshape(bsz, nb, w, A_KV_HEADS, A_GROUP, A_HEAD_DIM)
    s_loc = jnp.einsum('bnqhgd,bnkhd->bnhgqk', qb, kw).astype(jnp.float32) * scale
    qi = jnp.arange(w)[:, None]
    kj = jnp.arange(3 * w)[None, :]
    kpos = (jnp.arange(nb) * w)[:, None, None] - w + kj[None]
    mask = (jnp.abs(kj - w - qi) <= A_WINDOW)[None] & (kpos >= 0) & (kpos < t)
    s_loc = jnp.where(mask[None, :, None, None], s_loc, -jnp.inf)
    s_ctx = jnp.einsum('bnqhgd,bchd->bnhgqc', qb, k_ctx).astype(jnp.float32) * scale
    s_sink = jnp.broadcast_to(sink.astype(jnp.float32).reshape(A_KV_HEADS, A_GROUP, 1, 1), s_ctx.shape[:-1] + (1,))
    p = jax.nn.softmax(jnp.concatenate([s_loc, s_ctx, s_sink], axis=-1), axis=-1).astype(v.dtype)
    o = (jnp.einsum('bnhgqk,bnkhd->bnqhgd', p[..., :3 * w], vw)
         + jnp.einsum('bnhgqc,bchd->bnqhgd', p[..., 3 * w:-1], v_ctx))
    return o.reshape(bsz, t, A_Q)


def gqa_context(q, k, v, sink):
    bsz, tc = q.shape[:2]
    s = jnp.einsum('bqhgd,bkhd->bhgqk', q, k).astype(jnp.float32) * (A_HEAD_DIM ** -0.5)
    s_sink = jnp.broadcast_to(sink.astype(jnp.float32).reshape(A_KV_HEADS, A_GROUP, 1, 1), s.shape[:-1] + (1,))
    p = jax.nn.softmax(jnp.concatenate([s, s_sink], axis=-1), axis=-1)[..., :-1].astype(v.dtype)
    return jnp.einsum('bhgqk,bkhd->bqhgd', p, v).reshape(bsz, tc, A_Q)


def even_mixer(uc, ux, w_in, w_out, sink, conv_w, gate_b, norm_gain, rope, need_ctx):
    pc = split_cols(uc @ w_in, EVEN_SIZES)
    px = split_cols(ux @ w_in, EVEN_SIZES)
    bsz = ux.shape[0]

    def attn_heads(p):
        b_, t_ = p[0].shape[:2]
        return (p[0].reshape(b_, t_, A_KV_HEADS, A_GROUP, A_HEAD_DIM),
                p[1].reshape(b_, t_, A_KV_HEADS, A_HEAD_DIM),
                p[2].reshape(b_, t_, A_KV_HEADS, A_HEAD_DIM))
    qc, kc, vc = attn_heads(pc)
    qx, kx, vx = attn_heads(px)
    ya_x = windowed_gqa_latent(rope_2d(qx, rope), rope_2d(kx, rope), vx, kc, vc, sink)

    def mlstm_inputs(p):
        b_, t_ = p[3].shape[:2]
        q, k = jnp.split(jax.nn.silu(dwconv(p[3], conv_w)), 2, axis=-1)
        q = q.reshape(b_, t_, B_HEADS, B_QK_DIM).astype(jnp.float32) * (B_QK_DIM ** -0.5)
        k = k.reshape(b_, t_, B_HEADS, B_QK_DIM).astype(jnp.float32)
        v = p[4].reshape(b_, t_, B_HEADS, B_V_DIM).astype(jnp.float32)
        g = (p[6].astype(jnp.float32) + gate_b.astype(jnp.float32)).reshape(b_, t_, 4, B_HEADS)
        return tuple((q, k, v, g[:, :, d], jax.nn.log_sigmoid(g[:, :, 2 + d])) for d in range(2))

    zero = (jnp.zeros((bsz, B_HEADS, B_V_DIM, B_QK_DIM), jnp.float32),
            jnp.zeros((bsz, B_HEADS, B_QK_DIM), jnp.float32),
            jnp.zeros((bsz, B_HEADS), jnp.float32))
    hb_c, hb_x = bidir(mlstm_chunked, mlstm_inputs(pc), mlstm_inputs(px), zero, need_ctx)

    def mlstm_out(h, p):
        b_, t_ = p[5].shape[:2]
        o = jax.nn.sigmoid(p[5].reshape(b_, t_, B_HEADS, B_V_DIM))
        return (rmsnorm(h, norm_gain.reshape(B_HEADS, B_V_DIM)).astype(o.dtype) * o).reshape(b_, t_, B_V)

    yx = jnp.concatenate([ya_x, mlstm_out(hb_x, px)], axis=-1) @ w_out
    yc = None
    if need_ctx:
        ya_c = gqa_context(qc, kc, vc, sink)
        yc = jnp.concatenate([ya_c, mlstm_out(hb_c, pc)], axis=-1) @ w_out
    return yc, yx


def diff_attn(q, k, v, lam, scale):
    s = jnp.einsum('bqhmd,bkhmd->bhmqk', q, k).astype(jnp.float32) * scale
    p = jax.nn.softmax(s, axis=-1)
    a = (p[:, :, 0] - lam * p[:, :, 1]).astype(v.dtype)
    return jnp.einsum('bhqk,bkhv->bqhv', a, v)


def odd_mixer(uc, ux, w_in, w_out, lam_vecs, c_gain, conv_w, conv_b, dt_bias, a_log, skip, d_gain,
              rope, lam_init, need_ctx):
    pc = split_cols(uc @ w_in, ODD_SIZES)
    px = split_cols(ux @ w_in, ODD_SIZES)
    bsz, t = ux.shape[:2]

    def diff_heads(p):
        b_, t_ = p[0].shape[:2]
        return (p[0].reshape(b_, t_, C_HEADS, 2, C_HEAD_DIM),
                p[1].reshape(b_, t_, C_HEADS, 2, C_HEAD_DIM),
                p[2].reshape(b_, t_, C_HEADS, 2 * C_HEAD_DIM))
    qc, kc, vc = diff_heads(pc)
    qx, kx, vx = diff_heads(px)
    lv = lam_vecs.astype(jnp.float32)
    lam = jnp.exp(jnp.sum(lv[0] * lv[1])) - jnp.exp(jnp.sum(lv[2] * lv[3])) + lam_init
    scale = C_HEAD_DIM ** -0.5
    k_all = jnp.concatenate([rope_2d(kx, rope), kc], axis=1)
    v_all = jnp.concatenate([vx, vc], axis=1)
    nb = t // C_BLOCK
    qb = jnp.moveaxis(rope_2d(qx, rope).reshape(bsz, nb, C_BLOCK, C_HEADS, 2, C_HEAD_DIM), 1, 0)
    o_x = lax.map(lambda qblk: diff_attn(qblk, k_all, v_all, lam, scale), qb)
    o_x = jnp.moveaxis(o_x, 0, 1).reshape(bsz, t, C_HEADS, 2 * C_HEAD_DIM)

    def diff_out(o):
        return (rmsnorm(o, c_gain.reshape(C_HEADS, 2 * C_HEAD_DIM)) * (1.0 - lam_init)).reshape(
            o.shape[0], o.shape[1], C_V)

    hg = D_HEADS // D_GROUPS

    def ssd_inputs(p):
        b_, t_ = p[4].shape[:2]
        xbc = jax.nn.silu(dwconv(p[4], conv_w) + conv_b).astype(jnp.float32)
        xs, bm, cm = jnp.split(xbc, [D_INNER, D_INNER + D_GROUPS * D_STATE], axis=-1)
        xs = xs.reshape(b_, t_, D_GROUPS, hg, D_HEAD_DIM)
        bm = bm.reshape(b_, t_, D_GROUPS, D_STATE)
        cm = cm.reshape(b_, t_, D_GROUPS, D_STATE)
        dtr = p[5].astype(jnp.float32).reshape(b_, t_, 2, D_GROUPS, hg)
        dirs = []
        for d in range(2):
            dt = jax.nn.softplus(dtr[:, :, d] + dt_bias[d].astype(jnp.float32).reshape(D_GROUPS, hg))
            a = dt * (-jnp.exp(a_log[d].astype(jnp.float32))).reshape(D_GROUPS, hg)
            dirs.append((xs, dt, a, bm, cm))
        return xs, tuple(dirs)

    xs_c, dirs_c = ssd_inputs(pc)
    xs_x, dirs_x = ssd_inputs(px)
    zero = jnp.zeros((bsz, D_GROUPS, hg, D_HEAD_DIM, D_STATE), jnp.float32)
    ys_c, ys_x = bidir(ssd_chunked, dirs_c, dirs_x, zero, need_ctx)

    def ssd_out(y, xs, p):
        b_, t_ = p[3].shape[:2]
        y = y + skip.astype(jnp.float32).reshape(D_GROUPS, hg, 1) * xs
        z = jax.nn.silu(p[3].astype(jnp.float32))
        yz = y.reshape(b_, t_, D_GROUPS, hg * D_HEAD_DIM) * z.reshape(b_, t_, D_GROUPS, hg * D_HEAD_DIM)
        return rmsnorm(yz, d_gain.reshape(D_GROUPS, hg * D_HEAD_DIM)).reshape(b_, t_, D_INNER).astype(p[3].dtype)

    yx = jnp.concatenate([diff_out(o_x), ssd_out(ys_x, xs_x, px)], axis=-1) @ w_out
    yc = None
    if need_ctx:
        o_c = diff_attn(qc, kc, vc, lam, scale)
        yc = jnp.concatenate([diff_out(o_c), ssd_out(ys_c, xs_c, pc)], axis=-1) @ w_out
    return yc, yx


def conv_ffn(u, w_gate, w_up, conv_w, w_down):
    g = dwconv(u @ w_gate, conv_w)
    return (jax.nn.silu(g) * (u @ w_up)) @ w_down


def setup_inputs(seed: int = 0) -> dict:
    key = jax.random.key(seed)
    ks = jax.random.split(key, 32)
    f32 = jnp.float32
    D = D_MODEL
    ne, no = (DEPTH + 1) // 2, DEPTH // 2

    def nrm(i, shape, s):
        return jax.random.normal(ks[i], shape, f32) * s

    x = nrm(0, (BATCH, SEQ, D), 1.0)
    c = nrm(1, (BATCH, D), 1.0)
    ctx = nrm(2, (BATCH, CTX_LEN, D), 1.0)
    c_ctx = nrm(3, (D,), 1.0)
    mod_w = nrm(4, (DEPTH, D, 6 * D), 0.5 * D ** -0.5)
    mod_b = nrm(5, (DEPTH, 6 * D), 0.02)
    norm_g = 1.0 + nrm(6, (DEPTH, 4, D), 0.02)
    ffn_w_gate = nrm(7, (DEPTH, D, FFN_HIDDEN), D ** -0.5)
    ffn_w_up = nrm(8, (DEPTH, D, FFN_HIDDEN), D ** -0.5)
    ffn_conv = nrm(9, (DEPTH, FFN_CONV_W, FFN_HIDDEN), FFN_CONV_W ** -0.5)
    ffn_w_down = nrm(10, (DEPTH, FFN_HIDDEN, D), FFN_HIDDEN ** -0.5)
    ev_w_in = nrm(11, (ne, D, EVEN_IN), D ** -0.5)
    ev_w_out = nrm(12, (ne, MIX_WIDTH, D), MIX_WIDTH ** -0.5)
    a_sink = nrm(13, (ne, A_HEADS), 0.5)
    b_conv = nrm(14, (ne, SHORT_CONV_W, B_QK), SHORT_CONV_W ** -0.5)
    f_bias = jnp.broadcast_to(jnp.tile(jnp.linspace(3.0, 6.0, B_HEADS), 2), (ne, 2 * B_HEADS))
    b_gate_b = jnp.concatenate([nrm(15, (ne, 2 * B_HEADS), 0.1), f_bias + nrm(16, (ne, 2 * B_HEADS), 0.1)], axis=-1)
    b_norm_g = 1.0 + nrm(17, (ne, B_V), 0.02)
    od_w_in = nrm(18, (no, D, ODD_IN), D ** -0.5)
    od_w_out = nrm(19, (no, MIX_WIDTH, D), MIX_WIDTH ** -0.5)
    c_lambda = nrm(20, (no, 4, C_HEAD_DIM), 0.1)
    c_norm_g = 1.0 + nrm(21, (no, C_V), 0.02)
    d_conv = nrm(22, (no, SHORT_CONV_W, D_XBC), SHORT_CONV_W ** -0.5)
    d_conv_b = nrm(23, (no, D_XBC), 0.02)
    dt0 = jnp.exp(jax.random.uniform(ks[24], (no, 2, D_HEADS), f32, math.log(1e-3), math.log(1e-1)))
    d_dt_bias = dt0 + jnp.log(-jnp.expm1(-dt0))
    d_a_log = jnp.log(jax.random.uniform(ks[25], (no, 2, D_HEADS), f32, 1.0, 16.0))
    d_skip = 1.0 + nrm(26, (no, D_HEADS), 0.02)
    d_norm_g = 1.0 + nrm(27, (no, D_INNER), 0.02)
    return {'x': x, 'c': c, 'ctx': ctx, 'c_ctx': c_ctx, 'mod_w': mod_w, 'mod_b': mod_b, 'norm_g': norm_g,
            'ffn_w_gate': ffn_w_gate, 'ffn_w_up': ffn_w_up, 'ffn_conv': ffn_conv, 'ffn_w_down': ffn_w_down,
            'ev_w_in': ev_w_in, 'ev_w_out': ev_w_out, 'a_sink': a_sink, 'b_conv': b_conv, 'b_gate_b': b_gate_b,
            'b_norm_g': b_norm_g, 'od_w_in': od_w_in, 'od_w_out': od_w_out, 'c_lambda': c_lambda,
            'c_norm_g': c_norm_g, 'd_conv': d_conv, 'd_conv_b': d_conv_b, 'd_dt_bias': d_dt_bias,
            'd_a_log': d_a_log, 'd_skip': d_skip, 'd_norm_g': d_norm_g}


def reference(x, c, ctx, c_ctx, mod_w, mod_b, norm_g, ffn_w_gate, ffn_w_up, ffn_conv, ffn_w_down,
              ev_w_in, ev_w_out, a_sink, b_conv, b_gate_b, b_norm_g, od_w_in, od_w_out, c_lambda,
              c_norm_g, d_conv, d_conv_b, d_dt_bias, d_a_log, d_skip, d_norm_g):
    rows = x.shape[1] // GRID_W
    rope_a = axial_rope_tables(rows, A_HEAD_DIM)
    rope_c = axial_rope_tables(rows, C_HEAD_DIM)
    hx, hc = x, ctx
    for l in range(DEPTH):
        need_ctx = l < DEPTH - 1
        j = l // 2
        mx = jnp.split((jax.nn.silu(c) @ mod_w[l] + mod_b[l])[:, None, :], 6, axis=-1)
        mc = jnp.split(jax.nn.silu(c_ctx) @ mod_w[l] + mod_b[l], 6, axis=-1)
        g = norm_g[l]
        ux = modulate(rmsnorm(hx, g[0]), mx[0], mx[1])
        uc = modulate(rmsnorm(hc, g[0]), mc[0], mc[1])
        if l % 2 == 0:
            yc, yx = even_mixer(uc, ux, ev_w_in[j], ev_w_out[j], a_sink[j], b_conv[j], b_gate_b[j],
                                b_norm_g[j], rope_a, need_ctx)
        else:
            lam_init = 0.8 - 0.6 * math.exp(-0.3 * l)
            yc, yx = odd_mixer(uc, ux, od_w_in[j], od_w_out[j], c_lambda[j], c_norm_g[j], d_conv[j],
                               d_conv_b[j], d_dt_bias[j], d_a_log[j], d_skip[j], d_norm_g[j], rope_c,
                               lam_init, need_ctx)
        hx = hx + mx[2] * rmsnorm(yx, g[1])
        ux = modulate(rmsnorm(hx, g[2]), mx[3], mx[4])
        hx = hx + mx[5] * rmsnorm(conv_ffn(ux, ffn_w_gate[l], ffn_w_up[l], ffn_conv[l], ffn_w_down[l]), g[3])
        if need_ctx:
            hc = hc + mc[2] * rmsnorm(yc, g[1])
            uc = modulate(rmsnorm(hc, g[2]), mc[3], mc[4])
            hc = hc + mc[5] * rmsnorm(conv_ffn(uc, ffn_w_gate[l], ffn_w_up[l], ffn_conv[l], ffn_w_down[l]), g[3])
    return hx
```

```python
import math
from contextlib import ExitStack
import numpy as np
import ml_dtypes
import concourse.bass as bass
import concourse.mybir as mybir
from concourse.bass_utils import run_bass_kernel_spmd

F32 = mybir.dt.float32
BF16 = mybir.dt.bfloat16
AF = mybir.ActivationFunctionType
ALU = mybir.AluOpType
AX = mybir.AxisListType

ENGS = ("pe", "act", "dve", "pool", "sp")
NDSEM = {"sp": 24, "act": 8, "pool": 16}

D = 1024
TC = 256
TL = 4096
T = TC + TL
NT = T // 128
FH = 2816
EPS = 1e-6
NEG = -30000.0

class Buf:
    def __init__(self, name, n=1):
        self.name = name
        self.n = n
        self.w = [None] * n
        self.r = [[] for _ in range(n)]

    def __getitem__(self, k):
        if isinstance(k, slice):
            lo, hi, _ = k.indices(self.n)
            return (self, lo, hi)
        if k < 0:
            k += self.n
        return (self, k, k + 1)


def _acc(a):
    if isinstance(a, Buf):
        return (a, 0, a.n)
    return a


class Ins:
    __slots__ = ("eng", "idx", "fn", "cw", "dw", "signal", "is_dma", "dsem", "dval", "vc", "kd", "cnt", "ep")

    def __init__(self, eng, idx, fn):
        self.eng = eng
        self.idx = idx
        self.fn = fn
        self.cw = {}
        self.dw = []
        self.signal = False
        self.is_dma = False
        self.vc = None
        self.kd = None


class Prog:
    def __init__(self, nc):
        self.nc = nc
        self.q = {e: [] for e in ENGS}
        self.obs = {e: {x: -1 for x in ENGS} for e in ENGS}
        self.kdma = {e: set() for e in ENGS}
        self.ndma = {e: 0 for e in ENGS}
        self.dma_hist = {e: [] for e in ENGS}
        self.epoch = 0

    def op(self, eng, fn, reads=(), writes=(), dma=False):
        q = self.q[eng]
        ins = Ins(eng, len(q), fn)
        ins.is_dma = dma
        ins.ep = self.epoch
        deps = []
        for a in reads:
            b, lo, hi = _acc(a)
            for s in range(lo, hi):
                if b.w[s] is not None:
                    deps.append(b.w[s])
                if getattr(b, "excl", False):
                    for rd in b.r[s]:
                        if rd.eng != eng:
                            deps.append(rd)
        for a in writes:
            b, lo, hi = _acc(a)
            for s in range(lo, hi):
                if b.w[s] is not None:
                    deps.append(b.w[s])
                deps.extend(b.r[s])
        obs = self.obs[eng]
        kd = self.kdma[eng]
        if dma:
            k = self.ndma[eng]
            n = NDSEM[eng]
            ins.dsem = k % n
            ins.dval = 16 * (k // n + 1)
            if k >= n:
                deps.append(self.dma_hist[eng][k - n])
            self.ndma[eng] += 1
            self.dma_hist[eng].append(ins)
        for d in deps:
            if d.ep < self.epoch:
                continue
            if d.is_dma:
                if d not in kd and d not in ins.dw:
                    ins.dw.append(d)
                continue
            e = d.eng
            if obs[e] >= d.idx:
                continue
            if e == eng:
                if eng == "pe":
                    continue
                if eng != "pool" and ins.idx - d.idx > 2:
                    continue
            cur = ins.cw.get(e)
            if cur is None or cur.idx < d.idx:
                ins.cw[e] = d
        for d in ins.dw:
            kd.add(d)
            if d.kd:
                kd |= d.kd
            for e, v in d.vc.items():
                if obs[e] < v:
                    obs[e] = v
        for e, d in ins.cw.items():
            d.signal = True
            if obs[e] < d.idx:
                obs[e] = d.idx
            for e2, v in d.vc.items():
                if obs[e2] < v:
                    obs[e2] = v
            if d.kd:
                kd |= d.kd
        ins.vc = dict(obs)
        if not dma:
            ins.vc[eng] = max(ins.vc[eng], -1)
        ins.kd = set(kd) if len(kd) < 64 else None
        if len(kd) > 256:
            self.kdma[eng] = set(list(kd)[-128:])
        for a in reads:
            b, lo, hi = _acc(a)
            for s in range(lo, hi):
                b.r[s].append(ins)
        for a in writes:
            b, lo, hi = _acc(a)
            for s in range(lo, hi):
                b.w[s] = ins
                b.r[s] = []
        q.append(ins)
        return ins

    def setup(self, stack):
        nc = self.nc
        self.csem = {e: stack.enter_context(nc.semaphore("c_" + e)) for e in ENGS}
        self.dsem = {e: [stack.enter_context(nc.semaphore("d_%s%d" % (e, i))) for i in range(NDSEM[e])]
                     for e in NDSEM}
        self.emitted = {e: 0 for e in ENGS}
        self.bar_t = stack.enter_context(nc.sbuf_tensor("bar_t", [128, 8], F32))
        self.bar_d = nc.dram_tensor("bar_d", [2, 64], F32)
        self.bar_buf = Buf("bar", 8)
        self.cnt_run = {e: 0 for e in ENGS}
        self.nbar = 0

    def _last_real(self, e):
        for ins in reversed(self.q[e]):
            if not isinstance(ins, tuple):
                return ins
        return None

    def barrier(self):
        bt = self.bar_t
        marks = []
        for i, e in enumerate(("act", "dve", "pool")):
            if e == "act":
                fn = (lambda eng, i=i: eng.memzero(bt[:, i:i + 1]))
            else:
                fn = (lambda eng, i=i: eng.memset(bt[:, i:i + 1], 0.0))
            ins = self.op(e, fn, writes=[self.bar_buf[i]])
            if e in NDSEM:
                for d in self.dma_hist[e][-NDSEM[e]:]:
                    if d.ep == self.epoch and d not in self.kdma[e] and d not in ins.dw:
                        ins.dw.append(d)
            marks.append(ins)
        bd = self.bar_d
        src = self.bar_src if getattr(self, "bar_src", None) is not None else bd[0:1, :]
        ins = self.op("sp", (lambda eng: eng.dma_start(out=bd[1:2, :], in_=src)),
                      writes=[self.bar_buf[4]], dma=True)
        for d in self.dma_hist["sp"][-NDSEM["sp"] - 1:-1]:
            if d.ep == self.epoch and d not in self.kdma["sp"] and d not in ins.dw:
                ins.dw.append(d)
        marks.append(ins)
        pe_last = self._last_real("pe")
        if pe_last is not None:
            pe_last.signal = True
        for m in marks:
            m.signal = True
        for i, e in enumerate(("act", "dve", "pool")):
            j = 5 + i
            if e == "act":
                fn = (lambda eng, j=j: eng.memzero(bt[:, j:j + 1]))
            else:
                fn = (lambda eng, j=j: eng.memset(bt[:, j:j + 1], 0.0))
            ins = self.op(e, fn, writes=[self.bar_buf[j]])
            ins.cw = {m.eng: m for m in marks if not m.is_dma and m.eng != e}
            ins.dw = [m for m in marks if m.is_dma]
            if pe_last is not None:
                ins.cw["pe"] = pe_last
        self.q["pe"].append(("BAR", list(marks)))
        self.q["sp"].append(("BAR", list(marks) + ([pe_last] if pe_last else [])))
        self.epoch += 1
        for e in ENGS:
            self.kdma[e] = set()
            for x in ENGS:
                self.obs[e][x] = len(self.q[x]) - 1

    def emit(self, final=False):
        nc = self.nc
        csem, dsem = self.csem, self.dsem
        for e in ENGS:
            c = self.cnt_run[e]
            for ins in self.q[e][self.emitted[e]:]:
                if isinstance(ins, tuple):
                    continue
                if ins.signal and not ins.is_dma:
                    c += 1
                ins.cnt = c
            self.cnt_run[e] = c
        with nc.Block() as block:
            handles = {"pe": block.tensor, "act": block.scalar, "dve": block.vector,
                       "pool": block.gpsimd, "sp": block.sync}

            def make(e):
                def body(eng):
                    for ins in self.q[e][self.emitted[e]:]:
                        if isinstance(ins, tuple):
                            for d in ins[1]:
                                if d.is_dma:
                                    eng.wait_ge(dsem[d.eng][d.dsem], d.dval)
                                else:
                                    eng.wait_ge(csem[d.eng], d.cnt)
                            continue
                        for e2, d in ins.cw.items():
                            eng.wait_ge(csem[e2], d.cnt)
                        for d in ins.dw:
                            eng.wait_ge(dsem[d.eng][d.dsem], d.dval)
                        bi = ins.fn(eng)
                        if ins.is_dma:
                            bi.then_inc(dsem[e][ins.dsem], 16)
                        elif ins.signal:
                            bi.then_inc(csem[e], 1)
                    if final and e == "sp":
                        for qe in NDSEM:
                            last = {}
                            for d in self.dma_hist[qe]:
                                last[d.dsem] = d
                            for d in last.values():
                                eng.wait_ge(dsem[qe][d.dsem], d.dval)
                return body

            for e in ENGS:
                handles[e](make(e))
        for e in ENGS:
            self.emitted[e] = len(self.q[e])

    def stats(self):
        return {e: len(self.q[e]) for e in ENGS}


def host_consts():
    c = {}
    pos = np.arange(TL)
    row = (pos // 64).astype(np.float32)
    col = (pos % 64).astype(np.float32)
    nf = 16
    inv = (10000.0 ** (-np.arange(nf, dtype=np.float32) / nf)).astype(np.float32)
    ar = row[:, None] * inv
    ac = col[:, None] * inv
    cos = np.zeros((64, TL), np.float32)
    sin = np.zeros((64, TL), np.float32)
    for f in range(64):
        ang = ar if f < 32 else ac
        cos[f] = np.cos(ang[:, f % 16])
        sin[f] = np.sin(ang[:, f % 16])
    c["rope_cos"] = np.concatenate([cos, cos], 0)
    c["rope_sin"] = np.concatenate([sin, sin], 0)
    L = np.zeros((128, 128), np.float32)
    for f in range(128):
        if (f % 32) < 16:
            L[f + 16, f] = -1.0
        else:
            L[f - 16, f] = 1.0
    c["rope_perm"] = L
    c["ident_f"] = np.eye(128, dtype=np.float32)
    c["ident_h"] = np.eye(128).astype(ml_dtypes.bfloat16)
    kj = np.arange(128)[:, None]
    qi = np.arange(128)[None, :]
    c["mask_prev"] = (kj >= qi).astype(ml_dtypes.bfloat16)
    c["mask_next"] = (kj <= qi).astype(ml_dtypes.bfloat16)
    c["ones_f"] = np.ones((128, 128), np.float32)
    c["ones_h"] = np.ones((128, 128)).astype(ml_dtypes.bfloat16)
    tri_f = (kj <= qi).astype(np.float32)
    tri_b = (kj >= qi).astype(np.float32)
    c["tri"] = np.stack([tri_f, tri_b], 1).reshape(128, 256)
    c["mbias"] = np.stack([(1 - tri_f) * NEG, (1 - tri_b) * NEG], 1).reshape(128, 256).astype(np.float32)
    return c


CONST_DT = {"ident_h": BF16, "mask_prev": BF16, "mask_next": BF16, "ones_h": BF16}


INPUT_SPECS = {
    "x": ([TL, D], F32), "ctx": ([TC, D], F32), "ccT": ([D, 2], F32),
    "mod_w": ([2, D, 6 * D], F32), "mod_b": ([2, 6 * D], F32), "norm_g": ([2, 4, D], F32),
    "ffn_w_gate": ([2, D, FH], F32), "ffn_w_up": ([2, D, FH], F32), "ffn_convT": ([2, FH, 3], F32),
    "ffn_w_down": ([2, FH, D], F32),
    "ev_w_in": ([D, 2320], F32), "ev_w_out": ([D, D], F32), "a_sink": ([1, 8], F32),
    "b_convT": ([512, 5], F32), "b_gate_b": ([1, 16], F32), "b_norm_g": ([1, 512], F32),
    "od_w_in": ([D, 3088], F32), "od_w_out": ([D, D], F32), "c_lambda": ([4, 64], F32),
    "c_norm_g": ([1, 512], F32), "d_convT": ([1024, 5], F32), "d_conv_b": ([128, 8], F32),
    "d_dt_bias": ([1, 16], F32), "d_a_log": ([1, 16], F32), "d_skip": ([1, 8], F32), "d_norm_g": ([1, 512], F32),
}


class KB:
    def __init__(self, layers=(0, 1), debug=(), stop_after=None):
        self.nc = bass.Bass("TRN2", target_bir_lowering=False)
        self.P = Prog(self.nc)
        self.layers = tuple(layers)
        self.dbg = set(debug)
        self.stop_after = stop_after
        self.din = {}
        self.scr = {}
        self.sbuf_stack = None

    def inp(self, name, shape, dt=F32):
        t = self.nc.dram_tensor(name, list(shape), dt, kind="ExternalInput")
        self.din[name] = t
        return t

    def scratch(self, name, shape, dt, nslots=1):
        kind = "ExternalOutput" if name in self.dbg else "Internal"
        t = self.nc.dram_tensor(name, list(shape), dt, kind=kind)
        b = Buf(name, nslots)
        self.scr[name] = (t, b)
        return t, b

    def sb(self, st, name, shape, dt, nslots=1):
        self._uid = getattr(self, "_uid", 0) + 1
        name = "%s_u%d" % (name, self._uid)
        t = st.enter_context(self.nc.sbuf_tensor(name, list(shape), dt))
        return t, Buf(name, nslots)

    def ps(self, st, name, shape, dt=F32):
        self._uid = getattr(self, "_uid", 0) + 1
        name = "%s_u%d" % (name, self._uid)
        t = st.enter_context(self.nc.psum_tensor(name, list(shape), dt))
        b = Buf(name, 1)
        b.excl = True
        return t, b

    def dma(self, q, out, in_, reads=(), writes=()):
        return self.P.op(q, (lambda e: e.dma_start(out=out, in_=in_)), reads=reads, writes=writes, dma=True)

    def dma_nc(self, q, out, in_, reads=(), writes=()):
        return self.P.op(q, (lambda e: e.dma_start(out=out, in_=in_, allow_slow_non_contiguous=True)),
                         reads=reads, writes=writes, dma=True)

    def end_phase(self, final=False):
        if not final:
            self.P.barrier()
        self.P.emit(final=final)

    def load_consts(self, st, names):
        out = {}
        for n in names:
            src = self.din[n]
            shape = list(src.shape)
            t, b = self.sb(st, "c_" + n + "_%d" % self.P.epoch, shape, CONST_DT.get(n, F32))
            self.dma("sp", t[:], src.ap(), writes=[b])
            out[n] = (t, b)
        return out

    def build(self):
        nc = self.nc
        for n, (shape, dt) in INPUT_SPECS.items():
            self.inp(n, shape, dt)
        for n, a in host_consts().items():
            self.inp(n, a.shape, CONST_DT.get(n, F32))
        self.out = nc.dram_tensor("out", [TL, D], F32, kind="ExternalOutput")
        self.scratch("modv", [2, 2, 6, D], F32)
        if 0 in self.layers:
            self.scratch("hres", [T, D], F32, NT)
        else:
            self.scr["hres"] = (nc.dram_tensor("hres", [T, D], F32, kind="ExternalInput"), Buf("hres", NT))
        with ExitStack() as st:
            self.P.setup(st)
            self.P.bar_src = self.din["ones_f"][0:1, 0:64]
            self.phase_M()
            for l in (self.layers if self.stop_after != "M" else ()):
                if l == 0:
                    self.phase_P(0)
                    if self.stop_after == "P0":
                        break
                    self.scratch("CATT", [1024, T], BF16, NT)
                    self.phase_gqa(0)
                    if self.stop_after == "A0":
                        break
                    self.phase_mlstm(0)
                    if self.stop_after == "B0":
                        break
                    self.scratch("HMID", [T, D], F32, NT)
                    self.scratch("U2T", [8, 128, T], BF16, NT)
                    self.phase_W(0)
                    if self.stop_after == "W0":
                        break
                    self.phase_F(0, last=(self.layers[-1] == 0))
                    if self.stop_after == "F0":
                        break
                if l == 1:
                    self.phase_P(1)
                    if self.stop_after == "P1":
                        break
                    if "CATT" not in self.scr:
                        self.scratch("CATT", [1024, T], BF16, NT)
                    self.phase_diff(1)
                    if self.stop_after == "A1":
                        break
                    self.phase_ssd(1)
                    if self.stop_after == "B1":
                        break
                    if "HMID" not in self.scr:
                        self.scratch("HMID", [T, D], F32, NT)
                        self.scratch("U2T", [8, 128, T], BF16, NT)
                    self.phase_W(1)
                    if self.stop_after == "W1":
                        break
                    self.phase_F(1, last=True)
            self.end_phase(final=True)
        return nc

    def phase_M(self):
        P = self.P
        modv, modv_b = self.scr["modv"]
        with ExitStack() as st:
            cc, cc_b = self.sb(st, "cc", [128, 8, 2], F32)
            sc, sc_b = self.sb(st, "sc", [128, 8, 2], F32)
            wbl = [self.sb(st, "mw%d" % i, [128, 8, 512], F32) for i in range(2)]
            mv, mv_b = self.sb(st, "mv", [2, 6 * D], F32, 12)
            mb, mb_b = self.sb(st, "mb", [2, 6 * D], F32)
            ng, ng_b = self.sb(st, "ng", [2, 4, D], F32)
            res, res_b = self.sb(st, "res", [2, 6, D], F32)
            pss = [self.ps(st, "mps%d" % i, [2, 512]) for i in range(2)]
            self.dma("sp", cc[:], self.din["ccT"].rearrange("(kc p) m -> p kc m", p=128), writes=[cc_b])
            P.op("act", lambda e: e.activation(out=sc[:], in_=cc[:], func=AF.Silu), reads=[cc_b], writes=[sc_b])
            k = 0
            for l in self.layers:
                mw = self.din["mod_w"]
                self.dma("sp", mb[:], self.din["mod_b"][l:l + 1, :].broadcast_to([2, 6 * D]), writes=[mb_b])
                self.dma("sp", ng[:], self.din["norm_g"][l:l + 1, :, :].broadcast_to([2, 4, D]), writes=[ng_b])
                for cb in range(12):
                    wt, wb = wbl[k % 2]
                    pt, pb = pss[k % 2]
                    k += 1
                    src = mw[l].rearrange("(kc p) n -> p kc n", p=128)[:, :, cb * 512:(cb + 1) * 512]
                    self.dma("sp" if cb % 2 == 0 else "pool", wt[:], src, writes=[wb])
                    for kc in range(8):
                        P.op("pe", (lambda e, pt=pt, wt=wt, kc=kc: e.matmul(pt[:], lhsT=sc[:, kc, :], rhs=wt[:, kc, :],
                                                                      start=(kc == 0), stop=(kc == 7))),
                             reads=[sc_b, wb], writes=[pb])
                    P.op("dve", (lambda e, pt=pt, cb=cb: e.tensor_tensor(out=mv[:, cb * 512:(cb + 1) * 512], in0=pt[:],
                                                                     in1=mb[:, cb * 512:(cb + 1) * 512], op=ALU.add)),
                         reads=[pb, mb_b], writes=[mv_b[cb]])

                def sl(i):
                    return mv[:, i * D:(i + 1) * D]
                P.op("dve", lambda e: e.scalar_tensor_tensor(out=res[:, 0, :], in0=sl(1), scalar=1.0, in1=ng[:, 0, :],
                                                             op0=ALU.add, op1=ALU.mult), reads=[mv_b, ng_b], writes=[res_b])
                P.op("dve", lambda e: e.tensor_copy(out=res[:, 1, :], in_=sl(0)), reads=[mv_b], writes=[res_b])
                P.op("dve", lambda e: e.tensor_tensor(out=res[:, 2, :], in0=sl(2), in1=ng[:, 1, :], op=ALU.mult),
                     reads=[mv_b, ng_b], writes=[res_b])
                P.op("dve", lambda e: e.scalar_tensor_tensor(out=res[:, 3, :], in0=sl(4), scalar=1.0, in1=ng[:, 2, :],
                                                             op0=ALU.add, op1=ALU.mult), reads=[mv_b, ng_b], writes=[res_b])
                P.op("dve", lambda e: e.tensor_copy(out=res[:, 4, :], in_=sl(3)), reads=[mv_b], writes=[res_b])
                P.op("dve", lambda e: e.tensor_tensor(out=res[:, 5, :], in0=sl(5), in1=ng[:, 3, :], op=ALU.mult),
                     reads=[mv_b, ng_b], writes=[res_b])
                self.dma("sp", modv[l], res[:], reads=[res_b], writes=[modv_b])
            self.end_phase()

    def bc_load(self, st, name, l, who, idx):
        modv, modv_b = self.scr["modv"]
        t, b = self.sb(st, name, [128, D], F32)
        self.dma("sp", t[:], modv[l, who, idx:idx + 1, :].broadcast_to([128, D]), reads=[modv_b], writes=[b])
        return t, b

    def norm_to_uT(self, st, l, uT, uT_b, src_fn, idxA, idxS, tiles, ident, ident_b, psT):
        P = self.P
        A = [self.bc_load(st, "bcA%d" % w, l, w, idxA) for w in range(2)]
        S = [self.bc_load(st, "bcS%d" % w, l, w, idxS) for w in range(2)]
        hin = [self.sb(st, "hin%d" % i, [128, D], F32) for i in range(3)]
        tmp = [self.sb(st, "ntmp%d" % i, [128, D], F32) for i in range(2)]
        junk, junk_b = self.sb(st, "njunk", [128, D], F32)
        ubf = [self.sb(st, "ubf%d" % i, [128, D], BF16) for i in range(2)]
        stats, stats_b = self.sb(st, "nstats", [128, NT, 4], F32, NT)
        for n, i in enumerate(tiles):
            who = 1 if i < 2 else 0
            ht, hb = hin[n % 3]
            tt, tb = tmp[n % 2]
            ut, ub = ubf[n % 2]
            pT, pTb = psT[n % 2]
            self.dma("sp", ht[:], src_fn(i), writes=[hb])
            P.op("act", (lambda e, ht=ht, i=i: e.activation(out=junk[:], in_=ht[:], func=AF.Square,
                                                          accum_out=stats[:, i, 0:1])),
                 reads=[hb], writes=[junk_b, stats_b[i]])
            P.op("act", (lambda e, i=i: e.activation(out=stats[:, i, 1:2], in_=stats[:, i, 0:1], func=AF.Ln,
                                                     scale=1.0 / D, bias=EPS)), reads=[stats_b[i]], writes=[stats_b[i]])
            P.op("act", (lambda e, i=i: e.activation(out=stats[:, i, 2:3], in_=stats[:, i, 1:2], func=AF.Exp,
                                                     scale=-0.5)), reads=[stats_b[i]], writes=[stats_b[i]])
            P.op("dve", (lambda e, ht=ht, tt=tt, i=i, who=who: e.scalar_tensor_tensor(
                out=tt[:], in0=ht[:], scalar=stats[:, i, 2:3], in1=A[who][0][:], op0=ALU.mult, op1=ALU.mult)),
                reads=[hb, stats_b[i], A[who][1]], writes=[tb])
            P.op("dve", (lambda e, tt=tt, ut=ut, who=who: e.tensor_tensor(out=ut[:], in0=tt[:], in1=S[who][0][:],
                                                                          op=ALU.add)),
                 reads=[tb, S[who][1]], writes=[ub])
            for kc in range(8):
                P.op("pe", (lambda e, ut=ut, pT=pT, kc=kc: e.transpose(out=pT[:, kc * 128:(kc + 1) * 128],
                                                                    in_=ut[:, kc * 128:(kc + 1) * 128], identity=ident[:])),
                     reads=[ub, ident_b], writes=[pTb])
            eng = "act" if n % 2 == 0 else "dve"
            if eng == "act":
                P.op("act", (lambda e, pT=pT, i=i: e.copy(out=uT[:, :, i * 128:(i + 1) * 128],
                                                         in_=pT[:].rearrange("p (k t) -> p k t", k=8))),
                     reads=[pTb], writes=[uT_b[i]])
            else:
                P.op("dve", (lambda e, pT=pT, i=i: e.tensor_copy(out=uT[:, :, i * 128:(i + 1) * 128],
                                                                in_=pT[:].rearrange("p (k t) -> p k t", k=8))),
                     reads=[pTb], writes=[uT_b[i]])

    def tok_src0(self, i):
        if i < 2:
            return self.din["ctx"][i * 128:(i + 1) * 128, :]
        return self.din["x"][(i - 2) * 128:(i - 1) * 128, :]

    def phase_P(self, l):
        P = self.P
        even = (l % 2 == 0)
        W = self.din["ev_w_in" if even else "od_w_in"]
        NW = 2320 if even else 3088
        BLK = [(0, 256)] + [(256 + 512 * b, 512) for b in range(8)]
        if even:
            QAT, QAT_b = self.scratch("QAT", [512, T], BF16, 9)
            KDT, KDT_b = self.scratch("KDT", [2, 128, T], BF16, 9)
            VA, VA_b = self.scratch("VA", [128, NT, 2, 128], BF16, NT)
            QKBT, QKBT_b = self.scratch("QKBT", [512, T], BF16, 4)
            KBtm, KBtm_b = self.scratch("KBtm", [64, 68, 256], BF16, NT)
            VB, VB_b = self.scratch("VB", [64, 68, 516], BF16, NT)
            BO, BO_b = self.scratch("BO", [64, 68, 512], F32, NT)
            G, G_b = self.scratch("G", [64, 68, 16], F32, NT)
        else:
            self.scratch("QCT", [512, T], BF16, 9)
            self.scratch("KCT", [512, T], BF16, 9)
            self.scratch("XBCT", [1024, T], BF16, 8)
            self.scratch("XBtm", [128, NT, 768], BF16, NT)
            self.scratch("VC", [128, NT, 512], BF16, NT)
            self.scratch("Z", [128, NT, 512], F32, NT)
            self.scratch("DT", [128, NT, 16], F32, NT)
        with ExitStack() as so:
            uT, uT_b = self.sb(so, "uT", [128, 8, T], BF16, NT)
            w, w_b = self.sb(so, "w_in", [128, 8, NW], BF16, 8)
            for kc in range(8):
                self.dma("pool", w[:, kc, :], W[kc * 128:(kc + 1) * 128, :], writes=[w_b[kc]])
            cs = self.load_consts(so, ["ident_h", "rope_perm"])
            ident, ident_b = cs["ident_h"]
            perm, perm_b = cs["rope_perm"]
            psT = [self.ps(so, "psT%d" % i, [128, 1024], BF16) for i in range(2)]
            with ExitStack() as st:
                src_fn = self.tok_src0 if l == 0 else (lambda i: self.scr["hres"][0][i * 128:(i + 1) * 128, :])
                self.norm_to_uT(st, l, uT, uT_b, src_fn, 0, 1, list(range(NT)), ident, ident_b, psT)
                self.end_phase()
            with ExitStack() as st:
                if even:
                    self.proj_even(st, l, uT, uT_b, w, w_b, W, BLK, ident, ident_b, perm, perm_b, psT)
                else:
                    self.proj_odd(st, l, uT, uT_b, w, w_b, W, BLK, ident, ident_b, perm, perm_b, psT)
                self.end_phase()

    def proj_even(self, st, l, uT, uT_b, w, w_b, W, BLK, ident, ident_b, perm, perm_b, psT):
        P = self.P
        QAT, QAT_b = self.scr["QAT"]; KDT, KDT_b = self.scr["KDT"]; VA, VA_b = self.scr["VA"]
        QKBT, QKBT_b = self.scr["QKBT"]; KBtm, KBtm_b = self.scr["KBtm"]; VB, VB_b = self.scr["VB"]
        BO, BO_b = self.scr["BO"]; G, G_b = self.scr["G"]
        wk2, wk2_b = self.sb(st, "wk2", [128, 8, 2, 128], BF16)
        Wr = W.rearrange("(kc p) n -> p kc n", p=128)
        for h in range(2):
            for hf in range(2):
                self.dma("pool", wk2[:, :, h, hf * 64:(hf + 1) * 64], Wr[:, :, 512 + h * 64:512 + (h + 1) * 64],
                         writes=[wk2_b])
        pacc = [self.ps(st, "pacc%d" % i, [128, 512]) for i in range(2)]
        prope = [self.ps(st, "prope%d" % i, [128, 512]) for i in range(2)]
        ptm = [self.ps(st, "ptm%d" % i, [128, 512]) for i in range(2)]
        cosb = [self.sb(st, "cosb%d" % i, [128, 512], F32) for i in range(2)]
        sinb = [self.sb(st, "sinb%d" % i, [128, 512], F32) for i in range(2)]
        xs = [self.sb(st, "xs%d" % i, [128, 512], F32) for i in range(2)]
        t1 = [self.sb(st, "t1_%d" % i, [128, 512], F32) for i in range(2)]
        t2 = [self.sb(st, "t2_%d" % i, [128, 512], F32) for i in range(2)]
        obf = [self.sb(st, "obf%d" % i, [128, 512], BF16) for i in range(3)]
        n_acc = 0
        n_ob = 0
        chunks = [("q", c) for c in range(4)] + [("k", h) for h in range(2)]
        for bi, (t0, n) in enumerate(BLK):
            s0, s1 = t0 // 128, (t0 + n) // 128
            if bi > 0:
                ct, cb_ = cosb[bi % 2]; sn, sb_ = sinb[bi % 2]
                p0 = t0 - TC
                self.dma("sp", ct[:], self.din["rope_cos"][:, p0:p0 + 512], writes=[cb_])
                self.dma("sp", sn[:], self.din["rope_sin"][:, p0:p0 + 512], writes=[sb_])
            for kind, c in chunks:
                pa, pa_b = pacc[n_acc % 2]
                n_acc += 1
                for kc in range(8):
                    if kind == "q":
                        lf = (lambda kc=kc, c=c: w[:, kc, c * 128:(c + 1) * 128]); lb = [w_b[kc]]
                    else:
                        lf = (lambda kc=kc, c=c: wk2[:, kc, c, :]); lb = [wk2_b]
                    P.op("pe", (lambda e, pa=pa, lf=lf, kc=kc, t0=t0, n=n: e.matmul(
                        pa[:, :n], lhsT=lf(), rhs=uT[:, kc, t0:t0 + n], start=(kc == 0), stop=(kc == 7))),
                        reads=[uT_b[s0:s1]] + lb, writes=[pa_b])
                ob, ob_b = obf[n_ob % 3]
                n_ob += 1
                if bi == 0:
                    P.op("act", (lambda e, ob=ob, pa=pa, n=n: e.copy(out=ob[:, :n], in_=pa[:, :n])),
                         reads=[pa_b], writes=[ob_b])
                else:
                    x_, x_b = xs[n_acc % 2]; pr, pr_b = prope[n_acc % 2]
                    a_, a_b = t1[n_acc % 2]; b_, b_b = t2[n_acc % 2]
                    P.op("act", (lambda e, x_=x_, pa=pa: e.copy(out=x_[:], in_=pa[:])), reads=[pa_b], writes=[x_b])
                    P.op("pe", (lambda e, pr=pr, x_=x_: e.matmul(pr[:], lhsT=perm[:], rhs=x_[:], start=True, stop=True)),
                         reads=[perm_b, x_b], writes=[pr_b])
                    P.op("dve", (lambda e, a_=a_, x_=x_, ct=ct: e.tensor_tensor(out=a_[:], in0=x_[:], in1=ct[:], op=ALU.mult)),
                         reads=[x_b, cb_], writes=[a_b])
                    P.op("dve", (lambda e, b_=b_, pr=pr, sn=sn: e.tensor_tensor(out=b_[:], in0=pr[:], in1=sn[:], op=ALU.mult)),
                         reads=[pr_b, sb_], writes=[b_b])
                    P.op("dve", (lambda e, ob=ob, a_=a_, b_=b_: e.tensor_tensor(out=ob[:], in0=a_[:], in1=b_[:], op=ALU.add)),
                         reads=[a_b, b_b], writes=[ob_b])
                if kind == "q":
                    self.dma("sp", QAT[c * 128:(c + 1) * 128, t0:t0 + n], ob[:, :n], reads=[ob_b], writes=[QAT_b[bi]])
                else:
                    self.dma("sp", KDT[c, :, t0:t0 + n], ob[:, :n], reads=[ob_b], writes=[KDT_b[bi]])
        cw, cw_b = self.sb(st, "bconvw", [128, 4, 5], F32)
        self.dma("sp", cw[:], self.din["b_convT"].rearrange("(c p) k -> p c k", p=128), writes=[cw_b])
        pre, pre_b = self.sb(st, "pre", [128, T + 8], F32, 9)
        cacc, cacc_b = self.sb(st, "cacc", [128, T], F32, 2)
        sil, sil_b = self.sb(st, "sil", [128, T], BF16, NT)
        ktm = [self.sb(st, "ktm%d" % i, [128, 128], BF16) for i in range(2)]
        P.op("pool", lambda e: e.memset(pre[:], 0.0), writes=[pre_b])
        SEG = [(0, TC, 2), (TC, TL, 6)]
        for c in range(4):
            for bi, (t0, n) in enumerate(BLK):
                s0, s1 = t0 // 128, (t0 + n) // 128
                pa, pa_b = pacc[n_acc % 2]
                n_acc += 1
                for kc in range(8):
                    P.op("pe", (lambda e, pa=pa, kc=kc, c=c, t0=t0, n=n: e.matmul(
                        pa[:, :n], lhsT=w[:, kc, 768 + c * 128:768 + (c + 1) * 128], rhs=uT[:, kc, t0:t0 + n],
                        start=(kc == 0), stop=(kc == 7))), reads=[uT_b[s0:s1], w_b[kc]], writes=[pa_b])
                off = 2 if bi == 0 else 6
                if bi % 2 == 0:
                    P.op("act", (lambda e, pa=pa, t0=t0, n=n, off=off: e.copy(out=pre[:, t0 + off:t0 + off + n], in_=pa[:, :n])),
                         reads=[pa_b], writes=[pre_b[bi]])
                else:
                    P.op("dve", (lambda e, pa=pa, t0=t0, n=n, off=off: e.tensor_copy(out=pre[:, t0 + off:t0 + off + n], in_=pa[:, :n])),
                         reads=[pa_b], writes=[pre_b[bi]])
            for si, (ts, tn, off) in enumerate(SEG):
                eng = "dve"
                rb = [pre_b[0]] if si == 0 else [pre_b[1:9]]
                for j in range(5):
                    src0 = ts + off - 2 + j
                    if j == 0:
                        P.op(eng, (lambda e, c=c, ts=ts, tn=tn, src0=src0: e.tensor_scalar(
                            out=cacc[:, ts:ts + tn], in0=pre[:, src0:src0 + tn], scalar1=cw[:, c, 0:1], scalar2=None,
                            op0=ALU.mult)), reads=rb + [cw_b], writes=[cacc_b[si]])
                    else:
                        P.op(eng, (lambda e, c=c, j=j, ts=ts, tn=tn, src0=src0: e.scalar_tensor_tensor(
                            out=cacc[:, ts:ts + tn], in0=pre[:, src0:src0 + tn], scalar=cw[:, c, j:j + 1],
                            in1=cacc[:, ts:ts + tn], op0=ALU.mult, op1=ALU.add)),
                            reads=rb + [cw_b, cacc_b[si]], writes=[cacc_b[si]])
            P.op("act", lambda e: e.activation(out=sil[:, 0:TC], in_=cacc[:, 0:TC], func=AF.Silu),
                 reads=[cacc_b[0]], writes=[sil_b[0:2]])
            P.op("act", lambda e: e.activation(out=sil[:, TC:T], in_=cacc[:, TC:T], func=AF.Silu),
                 reads=[cacc_b[1]], writes=[sil_b[2:NT]])
            self.dma("sp", QKBT[c * 128:(c + 1) * 128, :], sil[:], reads=[sil_b], writes=[QKBT_b[c]])
            if c >= 2:
                for i in range(NT):
                    pT, pTb = psT[i % 2]
                    kt, kt_b = ktm[i % 2]
                    P.op("pe", (lambda e, pT=pT, i=i: e.transpose(out=pT[:, 0:128], in_=sil[:, i * 128:(i + 1) * 128],
                                                               identity=ident[:])), reads=[sil_b[i], ident_b], writes=[pTb])
                    P.op("dve", (lambda e, pT=pT, kt=kt: e.tensor_copy(out=kt[:], in_=pT[:, 0:128])), reads=[pTb], writes=[kt_b])
                    for hf in range(2):
                        self.dma("sp", KBtm[:, 2 * i + hf, (c - 2) * 128:(c - 1) * 128], kt[hf * 64:(hf + 1) * 64, :], reads=[kt_b],
                                 writes=[KBtm_b[i]])
        gb, gb_b = self.sb(st, "gateb", [128, 16], F32)
        self.dma("sp", gb[:], self.din["b_gate_b"][0:1, :].broadcast_to([128, 16]), writes=[gb_b])
        vast = [self.sb(st, "vast%d" % i, [128, 2, 128], BF16) for i in range(2)]
        for i in range(2):
            P.op("pool", (lambda e, i=i: e.memset(vast[i][0][:], 1.0)), writes=[vast[i][1]])
        vbst = [self.sb(st, "vbst%d" % i, [128, 4, 129], BF16) for i in range(2)]
        bost = [self.sb(st, "bost%d" % i, [128, 512], F32) for i in range(2)]
        gst = [self.sb(st, "gst%d" % i, [128, 16], F32) for i in range(2)]
        for i in range(2):
            P.op("pool", (lambda e, i=i: e.memset(vbst[i][0][:], 1.0)), writes=[vbst[i][1]])
        n_tm = 0
        for i in range(NT):
            lhs = lambda kc, i=i: uT[:, kc, i * 128:(i + 1) * 128]
            groups = [("bv", 1280, 512), ("bo", 1792, 512), ("va", 640, 128), ("g", 2304, 16)]
            for name, c0, cn in groups:
                pt, pt_b = ptm[n_tm % 2]
                n_tm += 1
                for kc in range(8):
                    P.op("pe", (lambda e, pt=pt, kc=kc, c0=c0, cn=cn, lhs=lhs: e.matmul(
                        pt[:, :cn], lhsT=lhs(kc), rhs=w[:, kc, c0:c0 + cn], start=(kc == 0), stop=(kc == 7))),
                        reads=[uT_b[i], w_b[kc]], writes=[pt_b])
                if name == "bv":
                    vt, vt_b = vbst[i % 2]
                    P.op("act", (lambda e, vt=vt, pt=pt: e.copy(out=vt[:, :, 0:128], in_=pt[:].rearrange("p (h d) -> p h d", h=4))),
                         reads=[pt_b], writes=[vt_b])
                    for hf in range(2):
                        self.dma("sp", VB[:, 2 * i + hf, :], vt[hf * 64:(hf + 1) * 64].rearrange("p h d -> p (h d)"), reads=[vt_b], writes=[VB_b[i]])
                elif name == "bo":
                    bt, bt_b = bost[i % 2]
                    P.op("act", (lambda e, bt=bt, pt=pt: e.activation(out=bt[:], in_=pt[:], func=AF.Sigmoid)), reads=[pt_b], writes=[bt_b])
                    for hf in range(2):
                        self.dma("sp", BO[:, 2 * i + hf, :], bt[hf * 64:(hf + 1) * 64, :], reads=[bt_b], writes=[BO_b[i]])
                elif name == "va":
                    at, at_b = vast[i % 2]
                    P.op("act", (lambda e, at=at, pt=pt: e.copy(out=at[:, :, 0:64], in_=pt[:, 0:128].rearrange("p (h d) -> p h d", h=2))),
                         reads=[pt_b], writes=[at_b])
                    self.dma("sp", VA[:, i, :, :], at[:], reads=[at_b], writes=[VA_b[i]])
                else:
                    gt, gt_b = gst[i % 2]
                    P.op("dve", (lambda e, gt=gt, pt=pt: e.tensor_tensor(out=gt[:], in0=pt[:, 0:16], in1=gb[:], op=ALU.add)),
                         reads=[pt_b, gb_b], writes=[gt_b])
                    for hf in range(2):
                        self.dma("sp", G[:, 2 * i + hf, :], gt[hf * 64:(hf + 1) * 64, :], reads=[gt_b], writes=[G_b[i]])


def make_in_maps(inputs, cores):
    consts = host_consts()
    f = lambda a: np.ascontiguousarray(np.asarray(a, dtype=np.float32))
    shared = {
        "mod_w": f(inputs["mod_w"]), "mod_b": f(inputs["mod_b"]), "norm_g": f(inputs["norm_g"]),
        "ffn_w_gate": f(inputs["ffn_w_gate"]), "ffn_w_up": f(inputs["ffn_w_up"]),
        "ffn_convT": f(np.transpose(inputs["ffn_conv"], (0, 2, 1))), "ffn_w_down": f(inputs["ffn_w_down"]),
        "ev_w_in": f(inputs["ev_w_in"][0]), "ev_w_out": f(inputs["ev_w_out"][0]), "a_sink": f(inputs["a_sink"]),
        "b_convT": f(inputs["b_conv"][0].T), "b_gate_b": f(inputs["b_gate_b"]), "b_norm_g": f(inputs["b_norm_g"]),
        "od_w_in": f(inputs["od_w_in"][0]), "od_w_out": f(inputs["od_w_out"][0]), "c_lambda": f(inputs["c_lambda"][0]),
        "c_norm_g": f(inputs["c_norm_g"]), "d_convT": f(inputs["d_conv"][0].T), "d_conv_b": f(inputs["d_conv_b"][0].reshape(8, 128).T),
        "d_dt_bias": f(inputs["d_dt_bias"]).reshape(1, 16), "d_a_log": f(inputs["d_a_log"]).reshape(1, 16),
        "d_skip": f(inputs["d_skip"]), "d_norm_g": f(inputs["d_norm_g"]),
    }
    shared.update(consts)
    maps = []
    for b in cores:
        m = dict(shared)
        m["x"] = f(inputs["x"][b])
        m["ctx"] = f(inputs["ctx"][b])
        m["ccT"] = f(np.stack([inputs["c"][b], inputs["c_ctx"]], axis=1))
        maps.append(m)
    return maps


def _gqa_phase(self, l):
    P = self.P
    QAT, QAT_b = self.scr["QAT"]; KDT, KDT_b = self.scr["KDT"]; VA, VA_b = self.scr["VA"]
    CATT, CATT_b = self.scr["CATT"]
    with ExitStack() as st:
        qa, qa_b = self.sb(st, "qa", [128, 4, T], BF16, 4)
        kd, kd_b = self.sb(st, "kd", [128, 2, 2, T], BF16, 2)
        P.op("pool", lambda e: e.memset(kd[:], 0.0), writes=[kd_b])
        va, va_b = self.sb(st, "vaug", [128, NT, 2, 128], BF16, 1)
        cs = self.load_consts(st, ["mask_prev", "mask_next"])
        mprev, mprev_b = cs["mask_prev"]; mnext, mnext_b = cs["mask_next"]
        for c in range(4):
            self.dma("sp", qa[:, c, :], QAT[c * 128:(c + 1) * 128, :], reads=[QAT_b], writes=[qa_b[c]])
        for h in range(2):
            for hf in range(2):
                self.dma("sp", kd[hf * 64:(hf + 1) * 64, h, hf, :], KDT[h, hf * 64:(hf + 1) * 64, :], reads=[KDT_b], writes=[kd_b[h]])
        self.dma("sp", va[:], VA.ap(), reads=[VA_b], writes=[va_b])
        es, es_b = self.sb(st, "esink", [128, 8], F32)
        self.dma("sp", es[:], self.din["a_sink"][0:1, :].broadcast_to([128, 8]), writes=[es_b])
        P.op("act", lambda e: e.activation(out=es[:], in_=es[:], func=AF.Exp), reads=[es_b], writes=[es_b])
        pS = [self.ps(st, "pS%d" % i, [128, 512]) for i in range(3)]
        pO = [self.ps(st, "pO%d" % i, [128, 512]) for i in range(2)]
        ET = [self.sb(st, "ET%d" % i, [128, 512], BF16) for i in range(4)]
        dn = [self.sb(st, "dn%d" % i, [128, 512], F32) for i in range(2)]
        rr = [self.sb(st, "rr%d" % i, [64, 512], F32) for i in range(2)]
        ost = [self.sb(st, "ost%d" % i, [64, 8, 128], BF16) for i in range(2)]
        qblocks = [("c", i) for i in range(2)] + [("x", n) for n in range(32)]
        units = []
        for qi, (kind, n) in enumerate(qblocks):
            if kind == "c":
                q0 = n * 128
                ktiles = [(0, None), (1, None)]
            else:
                q0 = TC + n * 128
                ktiles = [(0, None), (1, None)]
                if n >= 1:
                    ktiles.append((2 + n - 1, "prev"))
                ktiles.append((2 + n, None))
                if n <= 30:
                    ktiles.append((2 + n + 1, "next"))
            for h in range(2):
                for ki, (j, msk) in enumerate(ktiles):
                    units.append((qi, q0, h, ki, len(ktiles), j, msk))
        LA = 2
        NU = len(units)
        ets = {}
        for idx in range(NU + LA):
            if idx < NU:
                qi, q0, h, ki, nk, j, msk = units[idx]
                pS_, pS_b = pS[idx % 3]
                et, et_b = ET[idx % 4]
                ets[idx] = (et, et_b)
                for g in range(4):
                    c = 2 * h + g // 2
                    hf = g % 2
                    P.op("pe", (lambda e, pS_=pS_, g=g, c=c, hf=hf, j=j, h=h, q0=q0: e.matmul(
                        pS_[:, g * 128:(g + 1) * 128], lhsT=kd[:, h, hf, j * 128:(j + 1) * 128],
                        rhs=qa[:, c, q0:q0 + 128], start=True, stop=True)),
                        reads=[kd_b[h], qa_b[c]], writes=[pS_b])
                P.op("act", (lambda e, et=et, pS_=pS_: e.activation(out=et[:], in_=pS_[:], func=AF.Exp, scale=0.125)),
                     reads=[pS_b], writes=[et_b])
                if msk is not None:
                    mt, mt_b = (mprev, mprev_b) if msk == "prev" else (mnext, mnext_b)
                    eng = "dve"
                    P.op(eng, (lambda e, et=et, mt=mt: e.tensor_tensor(
                        out=et[:].rearrange("p (g t) -> p g t", g=4), in0=et[:].rearrange("p (g t) -> p g t", g=4),
                        in1=mt[:].rearrange("p (o t) -> p o t", o=1).broadcast_to([128, 4, 128]), op=ALU.mult)),
                        reads=[et_b, mt_b], writes=[et_b])
            k = idx - LA
            if k >= 0:
                qi, q0, h, ki, nk, j, msk = units[k]
                uo = qi * 2 + h
                po, po_b = pO[uo % 2]
                d_, d_b = dn[uo % 2]
                r_, r_b = rr[uo % 2]
                ot, ot_b = ost[qi % 2]
                et, et_b = ets.pop(k)
                P.op("pe", (lambda e, po=po, et=et, j=j, h=h, ki=ki, nk=nk: e.matmul(
                    po[:], lhsT=va[:, j, h, :], rhs=et[:], start=(ki == 0), stop=(ki == nk - 1))),
                    reads=[va_b, et_b], writes=[po_b])
                if ki == nk - 1:
                    P.op("dve", (lambda e, d_=d_, po=po, h=h: e.tensor_tensor(
                        out=d_[64:128, :].rearrange("p (g t) -> p g t", g=4), in0=po[64:128, :].rearrange("p (g t) -> p g t", g=4),
                        in1=es[64:128, h * 4:(h + 1) * 4].rearrange("p (g o) -> p g o", o=1).broadcast_to([64, 4, 128]), op=ALU.add)),
                        reads=[po_b, es_b], writes=[d_b])
                    P.op("dve", (lambda e, d_=d_, r_=r_: e.reciprocal(out=r_[:], in_=d_[64:128, :])), reads=[d_b], writes=[r_b])
                    P.op("dve", (lambda e, ot=ot, po=po, r_=r_, h=h: e.tensor_tensor(
                        out=ot[:, h * 4:(h + 1) * 4, :], in0=po[0:64, :].rearrange("p (g t) -> p g t", g=4),
                        in1=r_[:].rearrange("p (g t) -> p g t", g=4), op=ALU.mult)), reads=[po_b, r_b], writes=[ot_b])
                    if h == 1:
                        self.dma("sp", CATT[0:512, q0:q0 + 128].rearrange("(hd d) t -> d hd t", d=64), ot[:], reads=[ot_b],
                                 writes=[CATT_b[q0 // 128]])
        self.end_phase()


KB.phase_gqa = _gqa_phase


def _mlstm_phase(self, l):
    P = self.P
    QKBT, QKBT_b = self.scr["QKBT"]; KBtm, KBtm_b = self.scr["KBtm"]; VB, VB_b = self.scr["VB"]
    BO, BO_b = self.scr["BO"]; G, G_b = self.scr["G"]; CATT, CATT_b = self.scr["CATT"]
    L, C = 64, 68
    import os
    MM = int(os.environ.get("MLSTM_MODE", "9"))
    NCH = int(os.environ.get("MLSTM_NCH", "68"))
    GP = int(os.environ.get("MLSTM_GP", "9"))
    with ExitStack() as st:
        qT, qT_b = self.sb(st, "mqT", [128, 2, T], BF16, 2)
        kT, kT_b = self.sb(st, "mkT", [128, 2, T], BF16, 2)
        gt, gt_b = self.sb(st, "mgt", [64, C, 16], F32)
        cs = self.load_consts(st, ["tri", "mbias", "ones_f", "ident_f", "ident_h"])
        tri, tri_b = cs["tri"]; mbias, mbias_b = cs["mbias"]; ones_f, ones_b = cs["ones_f"]
        ident_f, identf_b = cs["ident_f"]; ident_h, identh_b = cs["ident_h"]
        for c in range(2):
            self.dma("sp", qT[:, c, :], QKBT[c * 128:(c + 1) * 128, :], reads=[QKBT_b], writes=[qT_b[c]])
        for c in range(2):
            self.dma("sp", kT[:, c, :], QKBT[256 + c * 128:256 + (c + 1) * 128, :], reads=[QKBT_b], writes=[kT_b[c]])
        self.dma("sp", gt[:], G.ap(), reads=[G_b], writes=[gt_b])
        ngt, ngt_b = self.sb(st, "mng", [64, 512], F32)
        self.dma("sp", ngt[:], self.din["b_norm_g"][0:1, :].broadcast_to([64, 512]), writes=[ngt_b])
        e1, e1_b = self.sb(st, "me1", [64, C, 8], F32)
        A, A_b = self.sb(st, "mA", [64, C, 8], F32)
        NB, NB_b = self.sb(st, "mNB", [64, C, 8], F32)
        tw, tw_b = e1, e1_b
        Wt, Wt_b = self.sb(st, "mWt", [64, C, 8], F32)
        ETOT, ETOT_b = self.sb(st, "mETOT", [128, C, 8], F32)
        pb = [self.ps(st, "mp%d" % i, [128, 512]) for i in range(7)]
        pT, pT_b = self.ps(st, "mpT", [128, 1024], BF16)
        if GP >= 1:
            P.op("act", lambda e: e.activation(out=e1[:], in_=gt[:, :, 8:16], func=AF.Exp, scale=-1.0), reads=[gt_b], writes=[e1_b])
        if GP >= 2:
            P.op("act", lambda e: e.activation(out=e1[:], in_=e1[:], func=AF.Ln, bias=1.0), reads=[e1_b], writes=[e1_b])
        if GP >= 3:
            P.op("dve", lambda e: e.tensor_scalar(out=A[:], in0=e1[:], scalar1=-1.0, scalar2=None, op0=ALU.mult), reads=[e1_b], writes=[A_b])
        for d in range(2 if GP >= 4 else 0):
            pcs, pcs_b = pb[2 * d]
            ptot, ptot_b = pb[2 * d + 1]
            P.op("pe", (lambda e, pcs=pcs, d=d: e.matmul(pcs[0:64, 0:C * 4], lhsT=tri[0:64, d * 128:d * 128 + 64],
                                                       rhs=A[:, :, d * 4:(d + 1) * 4], start=True, stop=True)),
                 reads=[tri_b, A_b], writes=[pcs_b])
            P.op("pe", (lambda e, ptot=ptot, d=d: e.matmul(ptot[:, 0:C * 4], lhsT=ones_f[0:64, :],
                                                         rhs=A[:, :, d * 4:(d + 1) * 4], start=True, stop=True)),
                 reads=[ones_b, A_b], writes=[ptot_b])
            G4 = int(os.environ.get("MLSTM_G4", "9"))
            if G4 < 2:
                continue
            P.op("dve", (lambda e, pcs=pcs, d=d: e.tensor_tensor(out=NB[:, :, d * 4:(d + 1) * 4], in0=gt[:, :, d * 4:(d + 1) * 4],
                                                               in1=pcs[0:64, 0:C * 4].rearrange("p (c j) -> p c j", j=4), op=ALU.subtract)),
                 reads=[gt_b, pcs_b], writes=[NB_b])
            if G4 < 3:
                continue
            P.op("dve", (lambda e, ptot=ptot, d=d: e.tensor_tensor(out=tw[:, :, d * 4:(d + 1) * 4], in0=NB[:, :, d * 4:(d + 1) * 4],
                                                                 in1=ptot[0:64, 0:C * 4].rearrange("p (c j) -> p c j", j=4), op=ALU.add)),
                 reads=[ptot_b, NB_b], writes=[tw_b])
            if G4 < 4:
                continue
            P.op("act", (lambda e, ptot=ptot, d=d: e.activation(out=ETOT[:, :, d * 4:(d + 1) * 4],
                                                              in_=ptot[:, 0:C * 4].rearrange("p (c j) -> p c j", j=4), func=AF.Exp)),
                 reads=[ptot_b], writes=[ETOT_b])
        if GP >= 5:
            P.op("act", lambda e: e.activation(out=Wt[:], in_=tw[:], func=AF.Exp), reads=[tw_b], writes=[Wt_b])
        Sst = [self.sb(st, "mS%d" % d, [128, C, 2, 129], BF16, C) for d in range(2)]
        run = [self.sb(st, "mrun%d" % d, [128, 2, 129], F32) for d in range(2)]
        order = [list(range(4)) + list(range(4, C)), list(range(3, -1, -1)) + list(range(C - 1, 3, -1))]
        NSB = 2
        vst = [[self.sb(st, "mv%d_%d" % (d, i), [64, 516], BF16) for i in range(NSB)] for d in range(2)]
        kst = [[self.sb(st, "mk%d_%d" % (d, i), [64, 256], BF16) for i in range(NSB)] for d in range(2)]
        vw = [[self.sb(st, "mvw%d_%d" % (d, i), [64, 516], BF16) for i in range(2)] for d in range(2)]
        for d in range(2):
            P.op("pool", (lambda e, d=d: e.memset(run[d][0][:], 0.0)), writes=[run[d][1]])
            P.op("pool", (lambda e, d=d: e.memset(Sst[d][0][:, order[d][0], :, :], 0.0)), writes=[Sst[d][1][order[d][0]]])
        for k in range(min(C, int(os.environ.get("MLSTM_SCK", "68"))) if MM >= 1 else 0):
            for d in range(2):
                c = order[d][k]
                vt, vt_b = vst[d][k % NSB]; kt, kt_b = kst[d][k % NSB]
                vwt, vwt_b = vw[d][k % 2]
                rt, rt_b = run[d]
                S_, S_b = Sst[d]
                QQ = os.environ.get("MLSTM_Q", "sp")
                self.dma(QQ, vt[:], VB[:, c, :], reads=[VB_b], writes=[vt_b])
                self.dma(QQ, kt[:], KBtm[:, c, :], reads=[KBtm_b], writes=[kt_b])
                for h in range(4 if int(os.environ.get("MLSTM_SC", "9")) >= 1 else 0):
                    P.op("dve", (lambda e, vwt=vwt, vt=vt, c=c, d=d, h=h: e.tensor_scalar(
                        out=vwt[:, h * 129:(h + 1) * 129], in0=vt[:, h * 129:(h + 1) * 129],
                        scalar1=Wt[:, c, d * 4 + h:d * 4 + h + 1], scalar2=None, op0=ALU.mult)),
                        reads=[vt_b, Wt_b], writes=[vwt_b])
                SC = int(os.environ.get("MLSTM_SC", "9"))
                for p in range(2 if SC >= 2 else 0):
                    pst, pst_b = pb[2 * d + p]
                    P.op("pe", (lambda e, pst=pst, kt=kt, vwt=vwt, p=p: e.matmul(
                        pst[:, 0:258], lhsT=kt[:, p * 128:(p + 1) * 128], rhs=vwt[:, p * 258:(p + 1) * 258], start=True, stop=True)),
                        reads=[kt_b, vwt_b], writes=[pst_b])
                    for hf in range(2 if SC >= 3 else 0):
                        j = d * 4 + 2 * p + hf
                        P.op("dve", (lambda e, rt=rt, pst=pst, p=p, hf=hf, j=j, c=c: e.scalar_tensor_tensor(
                            out=rt[hf * 64:(hf + 1) * 64, p, :], in0=rt[hf * 64:(hf + 1) * 64, p, :],
                            scalar=ETOT[hf * 64:(hf + 1) * 64, c, j:j + 1], in1=pst[hf * 64:(hf + 1) * 64, hf * 129:(hf + 1) * 129],
                            op0=ALU.mult, op1=ALU.add)), reads=[rt_b, pst_b, ETOT_b], writes=[rt_b])
                if k + 1 < C and SC >= 4:
                    cn = order[d][k + 1]
                    P.op("act", (lambda e, S_=S_, rt=rt, cn=cn: e.copy(out=S_[:, cn, :, :], in_=rt[:])), reads=[rt_b], writes=[S_b[cn]])
        TA = [self.sb(st, "mTA%d" % i, [64, 512], F32) for i in range(2)]
        X = [self.sb(st, "mX%d" % i, [64, 512], F32) for i in range(2)]
        Ebc = [self.sb(st, "mEbc%d" % i, [128, 512], F32) for i in range(2)]
        Edec = [self.sb(st, "mEdec%d" % i, [64, 512], F32) for i in range(2)]
        AT = [self.sb(st, "mAT%d" % i, [64, 512], BF16) for i in range(2)]
        qez = [self.sb(st, "mqez%d" % i, [128, 8, 64], BF16) for i in range(2)]
        vo = [self.sb(st, "mvo%d" % i, [64, 516], BF16) for i in range(3)]
        bo = [self.sb(st, "mbo%d" % i, [64, 512], F32) for i in range(4)]
        dab = [self.sb(st, "mdab%d" % i, [64, 8], F32) for i in range(2)]
        rc = [self.sb(st, "mrc%d" % i, [64, 8], F32) for i in range(2)]
        hb = [self.sb(st, "mhb%d" % i, [64, 8, 128], F32) for i in range(2)]
        hs = [self.sb(st, "mhs%d" % i, [64, 4, 128], F32) for i in range(2)]
        sq = [self.sb(st, "msq%d" % i, [64, 4, 128], F32) for i in range(1)] * 2
        ss = [self.sb(st, "mss%d" % i, [64, 8], F32) for i in range(2)]
        eo = [self.sb(st, "meo%d" % i, [64, 512], F32) for i in range(2)]
        y1 = [self.sb(st, "my1%d" % i, [64, 512], F32) for i in range(1)] * 2
        ybf = [self.sb(st, "mybf%d" % i, [64, 512], BF16) for i in range(2)]
        catst = [self.sb(st, "mcat%d" % i, [128, 4, 512], BF16, 8) for i in range(2)]
        qz0 = [self.sb(st, "mqz0%d" % i, [128, 4, 64], BF16) for i in range(2)]
        for i in range(2):
            P.op("pool", (lambda e, i=i: e.memset(qez[i][0][:], 0.0)), writes=[qez[i][1]])
            P.op("pool", (lambda e, i=i: e.memset(qz0[i][0][:], 0.0)), writes=[qz0[i][1]])
        trim = tri[0:64, :].rearrange("p (d t) -> p d t", d=2)[:, :, 0:64]
        mbm = mbias[0:64, :].rearrange("p (d t) -> p d t", d=2)[:, :, 0:64]
        pbc, pbc_b = pb[0]; pdec, pdec_b = pb[1]; pS2 = [pb[2], pb[6]]
        pout = [pb[3], pb[4], pb[5]]
        def stage(c, which):
            i2 = c % 2
            ta, ta_b = TA[i2]; x_, x_b = X[i2]; eb, eb_b = Ebc[i2]; ed, ed_b = Edec[i2]; at, at_b = AT[i2]
            qz, qz_b = qez[i2]; v_, v_b = vo[c % 3]; bo_, bo_b = bo[c % 4]
            da, da_b = dab[i2]; r_, r_b = rc[i2]; hb_, hb_b = hb[i2]; hs_, hs_b = hs[i2]; sq_, sq_b = sq[i2]
            ss_, ss_b = ss[i2]; eo_, eo_b = eo[i2]; y_, y_b = y1[i2]; yb, yb_b = ybf[i2]
            t0 = c * 64

            pS, pS_b = pS2[i2]
            if which == "A":
                self.dma("sp", v_[:], VB[:, c, :], reads=[VB_b], writes=[v_b])
                self.dma("sp", bo_[:], BO[:, c, :], reads=[BO_b], writes=[bo_b])
                P.op("dve", (lambda e, ta=ta, c=c: e.tensor_tensor(
                    out=ta[:].rearrange("p (d h t) -> p d h t", d=2, h=4),
                    in0=trim.rearrange("p d (o t) -> p d o t", o=1).broadcast_to([64, 2, 4, 64]),
                    in1=A[:, c, :].rearrange("p (d h o) -> p d h o", d=2, o=1).broadcast_to([64, 2, 4, 64]), op=ALU.mult)),
                    reads=[tri_b, A_b], writes=[ta_b])
                P.op("pool", (lambda e, x_=x_, c=c: e.tensor_tensor(
                    out=x_[:].rearrange("p (d h t) -> p d h t", d=2, h=4),
                    in0=mbm.rearrange("p d (o t) -> p d o t", o=1).broadcast_to([64, 2, 4, 64]),
                    in1=NB[:, c, :].rearrange("p (d h o) -> p d h o", d=2, o=1).broadcast_to([64, 2, 4, 64]), op=ALU.add)),
                    reads=[mbias_b, NB_b], writes=[x_b])
                P.op("pe", (lambda e, ta=ta: e.matmul(pbc[:], lhsT=ones_f[0:64, :], rhs=ta[:], start=True, stop=True)),
                     reads=[ones_b, ta_b], writes=[pbc_b])
                P.op("pe", (lambda e, ta=ta: e.matmul(pdec[0:64, :], lhsT=ones_f[0:64, 0:64], rhs=ta[:], start=True, stop=False)),
                     reads=[ones_b, ta_b], writes=[pdec_b])
                P.op("pe", (lambda e, x_=x_: e.matmul(pdec[0:64, :], lhsT=ident_f[0:64, 0:64], rhs=x_[:], start=False, stop=True)),
                     reads=[identf_b, x_b], writes=[pdec_b])
                P.op("act", (lambda e, eb=eb: e.activation(out=eb[:], in_=pbc[:], func=AF.Exp)), reads=[pbc_b], writes=[eb_b])
                P.op("act", (lambda e, ed=ed: e.activation(out=ed[:], in_=pdec[0:64, :], func=AF.Exp)), reads=[pdec_b], writes=[ed_b])
                q0, q0_b = qz0[i2]
                for hf in range(2):
                    P.op("act", (lambda e, q0=q0, hf=hf, t0=t0: e.copy(
                        out=q0[hf * 64:(hf + 1) * 64, :, :].rearrange("p (q f) t -> p q f t", f=2)[:, :, hf, :],
                        in_=qT[hf * 64:(hf + 1) * 64, :, t0:t0 + 64])), reads=[qT_b], writes=[q0_b])
                for h in range(4):
                    P.op("pe", (lambda e, h=h, t0=t0, q0=q0, pS=pS: e.matmul(pS[0:64, h * 64:(h + 1) * 64], lhsT=kT[:, h // 2, t0:t0 + 64],
                                                                     rhs=q0[:, h, :], start=True, stop=True)),
                         reads=[kT_b[h // 2], q0_b], writes=[pS_b])

            elif which == "B":
                P.op("dve", (lambda e, at=at, ed=ed, pS=pS: e.scalar_tensor_tensor(
                    out=at[:].rearrange("p (d x) -> p d x", d=2),
                    in0=pS[0:64, 0:256].rearrange("p (o x) -> p o x", o=1).broadcast_to([64, 2, 256]), scalar=0.125,
                    in1=ed[:].rearrange("p (d x) -> p d x", d=2), op0=ALU.mult, op1=ALU.mult)),
                    reads=[pS_b, ed_b], writes=[at_b])
                for hf in range(2):
                    for d in range(2):
                        P.op("dve", (lambda e, qz=qz, eb=eb, hf=hf, t0=t0, d=d: e.scalar_tensor_tensor(
                            out=qz[hf * 64:(hf + 1) * 64, :, :].rearrange("p (d q f) t -> p d q f t", d=2, q=2)[:, d, :, hf, :],
                            in0=qT[hf * 64:(hf + 1) * 64, :, t0:t0 + 64],
                            scalar=0.125,
                            in1=eb[hf * 64:(hf + 1) * 64, :].rearrange("p (d q f t) -> p d q f t", d=2, q=2, f=2)[:, d, :, hf, :],
                            op0=ALU.mult, op1=ALU.mult)), reads=[qT_b, eb_b], writes=[qz_b])
                for j in range(8):
                    d, h = j // 4, j % 4
                    po, po_b = pout[j // 3]
                    sl = j % 3
                    P.op("pe", (lambda e, po=po, sl=sl, at=at, v_=v_, j=j, h=h: e.matmul(
                        po[0:64, sl * 129:(sl + 1) * 129], lhsT=at[:, j * 64:(j + 1) * 64], rhs=v_[:, h * 129:(h + 1) * 129],
                        start=True, stop=False)), reads=[at_b, v_b], writes=[po_b])
                    P.op("pe", (lambda e, po=po, sl=sl, qz=qz, j=j, d=d, h=h, c=c: e.matmul(
                        po[0:64, sl * 129:(sl + 1) * 129], lhsT=qz[:, j, :], rhs=Sst[d][0][:, c, h // 2, :],
                        start=False, stop=True)), reads=[qz_b, Sst[d][1][c]], writes=[po_b])
                for b in range(3):
                    nb = 3 if b < 2 else 2
                    po, po_b = pout[b]
                    P.op("act", (lambda e, da=da, po=po, b=b, nb=nb: e.activation(
                        out=da[:, 3 * b:3 * b + nb], in_=po[0:64, 0:nb * 129].rearrange("p (n x) -> p n x", x=129)[:, :, 128],
                        func=AF.Abs)), reads=[po_b], writes=[da_b])
                P.op("dve", (lambda e, da=da: e.tensor_scalar(out=da[:], in0=da[:], scalar1=1.0, scalar2=None, op0=ALU.max)),
                     reads=[da_b], writes=[da_b])
                P.op("dve", (lambda e, da=da, r_=r_: e.reciprocal(out=r_[:], in_=da[:])), reads=[da_b], writes=[r_b])
                for b in range(3):
                    nb = 3 if b < 2 else 2
                    po, po_b = pout[b]
                    P.op("dve", (lambda e, hb_=hb_, po=po, r_=r_, b=b, nb=nb: e.tensor_tensor(
                        out=hb_[:, 3 * b:3 * b + nb, :], in0=po[0:64, 0:nb * 129].rearrange("p (n x) -> p n x", x=129)[:, :, 0:128],
                        in1=r_[:, 3 * b:3 * b + nb].rearrange("p (n o) -> p n o", o=1).broadcast_to([64, nb, 128]), op=ALU.mult)),
                        reads=[po_b, r_b], writes=[hb_b])

            if which == "C":
                P.op("dve", (lambda e, hs_=hs_, hb_=hb_: e.tensor_tensor(out=hs_[:], in0=hb_[:, 0:4, :], in1=hb_[:, 4:8, :], op=ALU.add)),
                     reads=[hb_b], writes=[hs_b])
                for h in range(4):
                    P.op("act", (lambda e, hs_=hs_, sq_=sq_, ss_=ss_, h=h: e.activation(out=sq_[:, h, :], in_=hs_[:, h, :], func=AF.Square,
                                                                                      accum_out=ss_[:, h:h + 1])),
                         reads=[hs_b], writes=[sq_b, ss_b])
                P.op("act", (lambda e, ss_=ss_: e.activation(out=ss_[:, 4:8], in_=ss_[:, 0:4], func=AF.Ln, scale=1.0 / 128, bias=EPS)),
                     reads=[ss_b], writes=[ss_b])
                P.op("act", (lambda e, ss_=ss_: e.activation(out=ss_[:, 0:4], in_=ss_[:, 4:8], func=AF.Exp, scale=-0.5)),
                     reads=[ss_b], writes=[ss_b])
            if which == "D":
                P.op("dve", (lambda e, y_=y_, hs_=hs_, ss_=ss_: e.tensor_tensor(
                    out=y_[:].rearrange("p (h x) -> p h x", h=4), in0=hs_[:],
                    in1=ss_[:, 0:4].rearrange("p (h o) -> p h o", o=1).broadcast_to([64, 4, 128]), op=ALU.mult)),
                    reads=[hs_b, ss_b], writes=[y_b])
                P.op("dve", (lambda e, y_=y_: e.tensor_tensor(out=y_[:], in0=y_[:], in1=ngt[:], op=ALU.mult)), reads=[y_b, ngt_b], writes=[y_b])
                P.op("dve", (lambda e, y_=y_, yb=yb, bo_=bo_: e.tensor_tensor(out=yb[:], in0=y_[:], in1=bo_[:], op=ALU.mult)),
                     reads=[y_b, bo_b], writes=[yb_b])
                cg = c // 8
                ct_, ct_b = catst[cg % 2]
                for kc in range(4):
                    P.op("pe", (lambda e, yb=yb, kc=kc: e.transpose(out=pT[:, kc * 64:(kc + 1) * 64], in_=yb[:, kc * 128:(kc + 1) * 128],
                                                                  identity=ident_h[0:64, 0:64])), reads=[yb_b, identh_b], writes=[pT_b])
                P.op("act", (lambda e, ct_=ct_, c=c: e.copy(out=ct_[:, :, (c % 8) * 64:(c % 8 + 1) * 64],
                                                           in_=pT[:, 0:256].rearrange("p (k t) -> p k t", k=4))),
                     reads=[pT_b], writes=[ct_b[c % 8]])
                if c % 8 == 7 or c == C - 1:
                    nt = ((c % 8) + 1) * 64
                    tk0 = cg * 512
                    self.dma("sp", CATT[512:1024, tk0:tk0 + nt].rearrange("(k p) t -> p k t", p=128), ct_[:, :, 0:nt],
                             reads=[ct_b], writes=[CATT_b[tk0 // 128:(tk0 + nt) // 128]])

        NC_ = min(NCH, C) if MM >= 2 else 0
        for s_ in range(NC_ + 3):
            if s_ < NC_:
                stage(s_, "A")
            if 0 <= s_ - 1 < NC_:
                stage(s_ - 1, "B")
            if 0 <= s_ - 2 < NC_:
                stage(s_ - 2, "C")
            if 0 <= s_ - 3 < NC_:
                stage(s_ - 3, "D")
        self.end_phase()


KB.phase_mlstm = _mlstm_phase


def _phase_W(self, l):
    P = self.P
    CATT, CATT_b = self.scr["CATT"]
    HMID, HMID_b = self.scr["HMID"]; U2T, U2T_b = self.scr["U2T"]
    Wo = self.din["ev_w_out" if l % 2 == 0 else "od_w_out"]
    tiles = list(range(NT)) if l == 0 else list(range(2, NT))
    src_fn = self.tok_src0 if l == 0 else (lambda i: self.scr["hres"][0][i * 128:(i + 1) * 128, :])
    with ExitStack() as st:
        catT, catT_b = self.sb(st, "catT", [128, 8, T], BF16, 8)
        wo, wo_b = self.sb(st, "wo", [128, 8, D], BF16, 8)
        for kc in range(8):
            self.dma("pool", wo[:, kc, :], Wo[kc * 128:(kc + 1) * 128, :], writes=[wo_b[kc]])
            c_lo = 0 if l == 0 else TC
            self.dma("sp", catT[:, kc, c_lo:], CATT[kc * 128:(kc + 1) * 128, c_lo:], reads=[CATT_b], writes=[catT_b[kc]])
        cs = self.load_consts(st, ["ident_h"])
        ident, ident_b = cs["ident_h"]
        G1 = [self.bc_load(st, "wG1_%d" % w, l, w, 2) for w in range(2)]
        A2 = [self.bc_load(st, "wA2_%d" % w, l, w, 3) for w in range(2)]
        S2 = [self.bc_load(st, "wS2_%d" % w, l, w, 4) for w in range(2)]
        hin = [self.sb(st, "whin%d" % i, [128, D], F32) for i in range(2)]
        tmp = [self.sb(st, "wtmp%d" % i, [128, D], F32) for i in range(2)]
        t2 = [self.sb(st, "wt2%d" % i, [128, D], F32) for i in range(2)]
        ubf = [self.sb(st, "wubf%d" % i, [128, D], BF16) for i in range(2)]
        u2st = [self.sb(st, "wu2st%d" % i, [128, 8, 512], BF16, 4) for i in range(2)]
        junk, junk_b = self.sb(st, "wjunk", [128, D], F32)
        stt, stt_b = self.sb(st, "wstat", [128, NT, 8], F32, NT)
        py = [[self.ps(st, "wpy%d_%d" % (i, hf), [128, 512]) for hf in range(2)] for i in range(2)]
        psT = [self.ps(st, "wpsT%d" % i, [128, 1024], BF16) for i in range(2)]
        for n, i in enumerate(tiles):
            who = 1 if i < 2 else 0
            ht, hb = hin[n % 2]; tt, tb = tmp[n % 2]; t2_, t2b = t2[n % 2]; ut, ub = ubf[n % 2]
            pT, pTb = psT[n % 2]
            us, us_b = u2st[(n // 4) % 2]
            self.dma("sp", ht[:], src_fn(i), writes=[hb])
            for hf in range(2):
                pt, pt_b = py[n % 2][hf]
                for kc in range(8):
                    P.op("pe", (lambda e, pt=pt, kc=kc, i=i, hf=hf: e.matmul(
                        pt[:], lhsT=catT[:, kc, i * 128:(i + 1) * 128], rhs=wo[:, kc, hf * 512:(hf + 1) * 512],
                        start=(kc == 0), stop=(kc == 7))), reads=[catT_b[kc], wo_b[kc]], writes=[pt_b])
                P.op("act", (lambda e, pt=pt, i=i, hf=hf: e.activation(out=junk[:, 0:512], in_=pt[:], func=AF.Square,
                                                                     accum_out=stt[:, i, hf:hf + 1])),
                     reads=[pt_b], writes=[junk_b, stt_b[i]])
            P.op("dve", (lambda e, i=i: e.tensor_tensor(out=stt[:, i, 2:3], in0=stt[:, i, 0:1], in1=stt[:, i, 1:2], op=ALU.add)),
                 reads=[stt_b[i]], writes=[stt_b[i]])
            P.op("act", (lambda e, i=i: e.activation(out=stt[:, i, 3:4], in_=stt[:, i, 2:3], func=AF.Ln, scale=1.0 / D, bias=EPS)),
                 reads=[stt_b[i]], writes=[stt_b[i]])
            P.op("act", (lambda e, i=i: e.activation(out=stt[:, i, 2:3], in_=stt[:, i, 3:4], func=AF.Exp, scale=-0.5)),
                 reads=[stt_b[i]], writes=[stt_b[i]])
            for hf in range(2):
                pt, pt_b = py[n % 2][hf]
                P.op("dve", (lambda e, pt=pt, tt=tt, i=i, hf=hf, who=who: e.scalar_tensor_tensor(
                    out=tt[:, hf * 512:(hf + 1) * 512], in0=pt[:], scalar=stt[:, i, 2:3], in1=G1[who][0][:, hf * 512:(hf + 1) * 512],
                    op0=ALU.mult, op1=ALU.mult)), reads=[pt_b, stt_b[i], G1[who][1]], writes=[tb])
            P.op("dve", (lambda e, tt=tt, ht=ht: e.tensor_tensor(out=tt[:], in0=tt[:], in1=ht[:], op=ALU.add)),
                 reads=[tb, hb], writes=[tb])
            self.dma("sp", HMID[i * 128:(i + 1) * 128, :], tt[:], reads=[tb], writes=[HMID_b[i]])
            P.op("act", (lambda e, tt=tt, i=i: e.activation(out=junk[:], in_=tt[:], func=AF.Square, accum_out=stt[:, i, 4:5])),
                 reads=[tb], writes=[junk_b, stt_b[i]])
            P.op("act", (lambda e, i=i: e.activation(out=stt[:, i, 5:6], in_=stt[:, i, 4:5], func=AF.Ln, scale=1.0 / D, bias=EPS)),
                 reads=[stt_b[i]], writes=[stt_b[i]])
            P.op("act", (lambda e, i=i: e.activation(out=stt[:, i, 6:7], in_=stt[:, i, 5:6], func=AF.Exp, scale=-0.5)),
                 reads=[stt_b[i]], writes=[stt_b[i]])
            P.op("dve", (lambda e, tt=tt, t2_=t2_, i=i, who=who: e.scalar_tensor_tensor(
                out=t2_[:], in0=tt[:], scalar=stt[:, i, 6:7], in1=A2[who][0][:], op0=ALU.mult, op1=ALU.mult)),
                reads=[tb, stt_b[i], A2[who][1]], writes=[t2b])
            P.op("dve", (lambda e, t2_=t2_, ut=ut, who=who: e.tensor_tensor(out=ut[:], in0=t2_[:], in1=S2[who][0][:], op=ALU.add)),
                 reads=[t2b, S2[who][1]], writes=[ub])
            for kc in range(8):
                P.op("pe", (lambda e, ut=ut, pT=pT, kc=kc: e.transpose(out=pT[:, kc * 128:(kc + 1) * 128],
                                                                    in_=ut[:, kc * 128:(kc + 1) * 128], identity=ident[:])),
                     reads=[ub, ident_b], writes=[pTb])
            q4 = n % 4
            P.op("act" if n % 2 == 0 else "dve",
                 (lambda e, pT=pT, us=us, q4=q4, n=n: (e.copy if n % 2 == 0 else e.tensor_copy)(
                     out=us[:, :, q4 * 128:(q4 + 1) * 128], **{("in_"): pT[:].rearrange("p (k t) -> p k t", k=8)})),
                 reads=[pTb], writes=[us_b[q4]])
            if q4 == 3 or n == len(tiles) - 1:
                i0 = tiles[n - q4]
                nt = (q4 + 1) * 128
                self.dma("sp", U2T[:, :, i0 * 128:i0 * 128 + nt].rearrange("k p t -> p k t"), us[:, :, 0:nt],
                         reads=[us_b[0:q4 + 1]], writes=[U2T_b[i0:i0 + q4 + 1]])
        self.end_phase()


def _phase_F(self, l, last):
    P = self.P
    HMID, HMID_b = self.scr["HMID"]; U2T, U2T_b = self.scr["U2T"]
    hres, hres_b = self.scr["hres"]
    Wg = self.din["ffn_w_gate"][l]; Wu = self.din["ffn_w_up"][l]; Wd = self.din["ffn_w_down"][l]
    NH = FH // 128
    blocks = ([(0, 256, 0, TC)] if l == 0 else []) + [(TC + 512 * b, 512, TC, T) for b in range(8)]
    with ExitStack() as st:
        wg, wg_b = self.sb(st, "fwg", [128, 8, FH], BF16, 4)
        wu, wu_b = self.sb(st, "fwu", [128, 8, FH], BF16, 4)
        wd, wd_b = self.sb(st, "fwd", [128, NH, D], BF16, 2)
        Wgr = Wg.rearrange("(kc p) n -> p kc n", p=128); Wur = Wu.rearrange("(kc p) n -> p kc n", p=128)
        Wdr = Wd.rearrange("(hc p) n -> p hc n", p=128)
        for g in range(4):
            self.dma("pool", wg[:, :, g * 704:(g + 1) * 704], Wgr[:, :, g * 704:(g + 1) * 704], writes=[wg_b[g]])
            self.dma("pool", wu[:, :, g * 704:(g + 1) * 704], Wur[:, :, g * 704:(g + 1) * 704], writes=[wu_b[g]])
        for g in range(2):
            self.dma("pool", wd[:, g * 11:(g + 1) * 11, :], Wdr[:, g * 11:(g + 1) * 11, :], writes=[wd_b[g]])
        cw, cw_b = self.sb(st, "fcw", [128, NH, 3], F32)
        self.dma("sp", cw[:], self.din["ffn_convT"][l].rearrange("(hc p) k -> p hc k", p=128), writes=[cw_b])
        G2 = [self.bc_load(st, "fG2_%d" % w, l, w, 5) for w in range(2)]
        ub, ub_b = self.sb(st, "fub", [128, 8, 514], BF16)
        P.op("pool", lambda e: e.memset(ub[:], 0.0), writes=[ub_b])
        AT, AT_b = self.sb(st, "fAT", [128, NH, 512], BF16, NH)
        Gs = [self.sb(st, "fGs%d" % i, [128, 514], F32) for i in range(2)]
        cv, cv_b = self.sb(st, "fcv", [128, 512], F32)
        sl, sl_b = self.sb(st, "fsl", [128, 512], F32)
        upc, upc_b = self.sb(st, "fupc", [128, 512], F32)
        junk, junk_b = self.sb(st, "fjunk", [128, 512], F32)
        hm = [self.sb(st, "fhm%d" % i, [128, D], F32) for i in range(2)]
        tmp = [self.sb(st, "ftmp%d" % i, [128, D], F32) for i in range(2)]
        stt, stt_b = self.sb(st, "fstat", [128, NT, 4], F32, NT)
        pg = [self.ps(st, "fpg%d" % i, [128, 512]) for i in range(2)]
        pu = [self.ps(st, "fpu%d" % i, [128, 512]) for i in range(2)]
        phs = [self.ps(st, "fph%d" % i, [128, 512]) for i in range(2)]
        pd = [self.ps(st, "fpd%d" % i, [128, 512]) for i in range(2)]
        nq = 0
        for (t0, n, s0, s1) in blocks:
            hl = t0 > s0
            hr = t0 + n < s1
            lo = t0 - 1 if hl else t0
            hi = t0 + n + 1 if hr else t0 + n
            c0 = 0 if hl else 1
            self.dma("sp", ub[:, :, c0:c0 + (hi - lo)], U2T[:, :, lo:hi].rearrange("k p t -> p k t"), reads=[U2T_b], writes=[ub_b])
            for hc in range(NH):
                g = (hc * 128) // 704
                g2 = (hc * 128 + 127) // 704
                pg_, pg_b = pg[hc % 2]; pu_, pu_b = pu[hc % 2]; gs, gs_b = Gs[hc % 2]
                ph, ph_b = phs[hc % 2]
                for kc in range(8):
                    P.op("pe", (lambda e, pg_=pg_, kc=kc, hc=hc, n=n: e.matmul(
                        pg_[:, :n], lhsT=wg[:, kc, hc * 128:(hc + 1) * 128], rhs=ub[:, kc, 1:n + 1], start=(kc == 0), stop=(kc == 7))),
                        reads=[wg_b[g:g2 + 1], ub_b], writes=[pg_b])
                for kc in range(8):
                    P.op("pe", (lambda e, kc=kc, hc=hc, n=n, ph=ph: e.matmul(
                        ph[:, 0:2], lhsT=wg[:, kc, hc * 128:(hc + 1) * 128], rhs=ub[:, kc, 0:n + 2:n + 1], start=(kc == 0), stop=(kc == 7))),
                        reads=[wg_b[g:g2 + 1], ub_b], writes=[ph_b])
                for kc in range(8):
                    P.op("pe", (lambda e, pu_=pu_, kc=kc, hc=hc, n=n: e.matmul(
                        pu_[:, :n], lhsT=wu[:, kc, hc * 128:(hc + 1) * 128], rhs=ub[:, kc, 1:n + 1], start=(kc == 0), stop=(kc == 7))),
                        reads=[wu_b[g:g2 + 1], ub_b], writes=[pu_b])
                P.op("act", (lambda e, gs=gs, pg_=pg_, n=n: e.copy(out=gs[:, 1:n + 1], in_=pg_[:, :n])), reads=[pg_b], writes=[gs_b])
                P.op("dve", (lambda e, gs=gs, n=n, ph=ph: e.tensor_copy(out=gs[:, 0:n + 2:n + 1], in_=ph[:, 0:2])), reads=[ph_b], writes=[gs_b])
                if not hl:
                    P.op("pool", (lambda e, gs=gs: e.memset(gs[:, 0:1], 0.0)), writes=[gs_b])
                if not hr:
                    P.op("pool", (lambda e, gs=gs, n=n: e.memset(gs[:, n + 1:n + 2], 0.0)), writes=[gs_b])
                P.op("dve", (lambda e, gs=gs, hc=hc, n=n: e.tensor_scalar(out=cv[:, :n], in0=gs[:, 0:n], scalar1=cw[:, hc, 0:1],
                                                                         scalar2=None, op0=ALU.mult)), reads=[gs_b, cw_b], writes=[cv_b])
                for j in (1, 2):
                    P.op("dve", (lambda e, gs=gs, hc=hc, n=n, j=j: e.scalar_tensor_tensor(
                        out=cv[:, :n], in0=gs[:, j:j + n], scalar=cw[:, hc, j:j + 1], in1=cv[:, :n], op0=ALU.mult, op1=ALU.add)),
                        reads=[gs_b, cw_b, cv_b], writes=[cv_b])
                P.op("act", (lambda e, n=n: e.activation(out=sl[:, :n], in_=cv[:, :n], func=AF.Silu)), reads=[cv_b], writes=[sl_b])
                P.op("dve", (lambda e, hc=hc, n=n, pu_=pu_: e.tensor_tensor(out=AT[:, hc, :n], in0=pu_[:, :n], in1=sl[:, :n], op=ALU.mult)),
                     reads=[sl_b, pu_b], writes=[AT_b[hc]])
            for tt in range(n // 128):
                i = t0 // 128 + tt
                who = 1 if i < 2 else 0
                hm_, hm_b = hm[nq % 2]; tp, tp_b = tmp[nq % 2]
                nq += 1
                self.dma("sp", hm_[:], HMID[i * 128:(i + 1) * 128, :], reads=[HMID_b[i]], writes=[hm_b])
                for hf in range(2):
                    pd_, pd_b = pd[hf]
                    for hc in range(NH):
                        P.op("pe", (lambda e, pd_=pd_, hc=hc, tt=tt, hf=hf: e.matmul(
                            pd_[:], lhsT=AT[:, hc, tt * 128:(tt + 1) * 128], rhs=wd[:, hc, hf * 512:(hf + 1) * 512],
                            start=(hc == 0), stop=(hc == NH - 1))), reads=[AT_b[hc], wd_b[hc // 11]], writes=[pd_b])
                    P.op("act", (lambda e, pd_=pd_, i=i, hf=hf: e.activation(out=junk[:], in_=pd_[:], func=AF.Square,
                                                                           accum_out=stt[:, i, hf:hf + 1])),
                         reads=[pd_b], writes=[junk_b, stt_b[i]])
                P.op("dve", (lambda e, i=i: e.tensor_tensor(out=stt[:, i, 2:3], in0=stt[:, i, 0:1], in1=stt[:, i, 1:2], op=ALU.add)),
                     reads=[stt_b[i]], writes=[stt_b[i]])
                P.op("act", (lambda e, i=i: e.activation(out=stt[:, i, 3:4], in_=stt[:, i, 2:3], func=AF.Ln, scale=1.0 / D, bias=EPS)),
                     reads=[stt_b[i]], writes=[stt_b[i]])
                P.op("act", (lambda e, i=i: e.activation(out=stt[:, i, 2:3], in_=stt[:, i, 3:4], func=AF.Exp, scale=-0.5)),
                     reads=[stt_b[i]], writes=[stt_b[i]])
                for hf in range(2):
                    pd_, pd_b = pd[hf]
                    P.op("dve", (lambda e, pd_=pd_, tp=tp, i=i, hf=hf, who=who: e.scalar_tensor_tensor(
                        out=tp[:, hf * 512:(hf + 1) * 512], in0=pd_[:], scalar=stt[:, i, 2:3], in1=G2[who][0][:, hf * 512:(hf + 1) * 512],
                        op0=ALU.mult, op1=ALU.mult)), reads=[pd_b, stt_b[i], G2[who][1]], writes=[tp_b])
                P.op("dve", (lambda e, tp=tp, hm_=hm_: e.tensor_tensor(out=tp[:], in0=tp[:], in1=hm_[:], op=ALU.add)),
                     reads=[tp_b, hm_b], writes=[tp_b])
                if last and i >= 2:
                    self.dma("sp", self.out[(i - 2) * 128:(i - 1) * 128, :], tp[:], reads=[tp_b], writes=[hres_b[i]])
                else:
                    self.dma("sp", hres[i * 128:(i + 1) * 128, :], tp[:], reads=[tp_b], writes=[hres_b[i]])
        self.end_phase()


KB.phase_W = _phase_W
KB.phase_F = _phase_F


def _proj_odd(self, st, l, uT, uT_b, w, w_b, W, BLK, ident, ident_b, perm, perm_b, psT):
    P = self.P
    QCT, QCT_b = self.scr["QCT"]; KCT, KCT_b = self.scr["KCT"]; XBCT, XBCT_b = self.scr["XBCT"]
    XBtm, XBtm_b = self.scr["XBtm"]; VC, VC_b = self.scr["VC"]; Z, Z_b = self.scr["Z"]; DT, DT_b = self.scr["DT"]
    pacc = [self.ps(st, "pacc%d" % i, [128, 512]) for i in range(2)]
    prope = [self.ps(st, "prope%d" % i, [128, 512]) for i in range(2)]
    ptm = [self.ps(st, "ptm%d" % i, [128, 512]) for i in range(2)]
    cosb = [self.sb(st, "cosb%d" % i, [128, 512], F32) for i in range(2)]
    sinb = [self.sb(st, "sinb%d" % i, [128, 512], F32) for i in range(2)]
    xs = [self.sb(st, "xs%d" % i, [128, 512], F32) for i in range(2)]
    t1 = [self.sb(st, "t1_%d" % i, [128, 512], F32) for i in range(2)]
    t2 = [self.sb(st, "t2_%d" % i, [128, 512], F32) for i in range(2)]
    obf = [self.sb(st, "obf%d" % i, [128, 512], BF16) for i in range(3)]
    n_acc = 0
    n_ob = 0
    chunks = [("q", c) for c in range(4)] + [("k", c) for c in range(4)]
    for bi, (t0, n) in enumerate(BLK):
        s0, s1 = t0 // 128, (t0 + n) // 128
        if bi > 0:
            ct, cb_ = cosb[bi % 2]; sn, sb_ = sinb[bi % 2]
            p0 = t0 - TC
            self.dma("sp", ct[:], self.din["rope_cos"][:, p0:p0 + 512], writes=[cb_])
            self.dma("sp", sn[:], self.din["rope_sin"][:, p0:p0 + 512], writes=[sb_])
        for kind, c in chunks:
            col0 = (0 if kind == "q" else 512) + c * 128
            pa, pa_b = pacc[n_acc % 2]
            n_acc += 1
            for kc in range(8):
                P.op("pe", (lambda e, pa=pa, kc=kc, col0=col0, t0=t0, n=n: e.matmul(
                    pa[:, :n], lhsT=w[:, kc, col0:col0 + 128], rhs=uT[:, kc, t0:t0 + n], start=(kc == 0), stop=(kc == 7))),
                    reads=[uT_b[s0:s1], w_b[kc]], writes=[pa_b])
            ob, ob_b = obf[n_ob % 3]
            n_ob += 1
            if bi == 0:
                P.op("act", (lambda e, ob=ob, pa=pa, n=n: e.copy(out=ob[:, :n], in_=pa[:, :n])), reads=[pa_b], writes=[ob_b])
            else:
                x_, x_b = xs[n_acc % 2]; pr, pr_b = prope[n_acc % 2]
                a_, a_b = t1[n_acc % 2]; b_, b_b = t2[n_acc % 2]
                P.op("act", (lambda e, x_=x_, pa=pa: e.copy(out=x_[:], in_=pa[:])), reads=[pa_b], writes=[x_b])
                P.op("pe", (lambda e, pr=pr, x_=x_: e.matmul(pr[:], lhsT=perm[:], rhs=x_[:], start=True, stop=True)),
                     reads=[perm_b, x_b], writes=[pr_b])
                P.op("dve", (lambda e, a_=a_, x_=x_, ct=ct: e.tensor_tensor(out=a_[:], in0=x_[:], in1=ct[:], op=ALU.mult)),
                     reads=[x_b, cb_], writes=[a_b])
                P.op("dve", (lambda e, b_=b_, pr=pr, sn=sn: e.tensor_tensor(out=b_[:], in0=pr[:], in1=sn[:], op=ALU.mult)),
                     reads=[pr_b, sb_], writes=[b_b])
                P.op("dve", (lambda e, ob=ob, a_=a_, b_=b_: e.tensor_tensor(out=ob[:], in0=a_[:], in1=b_[:], op=ALU.add)),
                     reads=[a_b, b_b], writes=[ob_b])
            dst, dst_b = (QCT, QCT_b) if kind == "q" else (KCT, KCT_b)
            self.dma("sp", dst[c * 128:(c + 1) * 128, t0:t0 + n], ob[:, :n], reads=[ob_b], writes=[dst_b[bi]])
    cw, cw_b = self.sb(st, "dconvw", [128, 8, 5], F32)
    self.dma("sp", cw[:], self.din["d_convT"].rearrange("(c p) k -> p c k", p=128), writes=[cw_b])
    cbias, cbias_b = self.sb(st, "dconvb", [128, 8], F32)
    self.dma("sp", cbias[:], self.din["d_conv_b"].ap(), writes=[cbias_b])
    pre, pre_b = self.sb(st, "pre", [128, T + 8], F32, 9)
    cacc, cacc_b = self.sb(st, "cacc", [128, T], F32, 2)
    sil, sil_b = self.sb(st, "sil", [128, T], BF16, NT)
    ktm = [self.sb(st, "ktm%d" % i, [128, 128], BF16) for i in range(2)]
    P.op("pool", lambda e: e.memset(pre[:], 0.0), writes=[pre_b])
    SEG = [(0, TC, 2), (TC, TL, 6)]
    for c in range(8):
        for bi, (t0, n) in enumerate(BLK):
            s0, s1 = t0 // 128, (t0 + n) // 128
            pa, pa_b = pacc[n_acc % 2]
            n_acc += 1
            for kc in range(8):
                P.op("pe", (lambda e, pa=pa, kc=kc, c=c, t0=t0, n=n: e.matmul(
                    pa[:, :n], lhsT=w[:, kc, 2048 + c * 128:2048 + (c + 1) * 128], rhs=uT[:, kc, t0:t0 + n],
                    start=(kc == 0), stop=(kc == 7))), reads=[uT_b[s0:s1], w_b[kc]], writes=[pa_b])
            off = 2 if bi == 0 else 6
            if bi % 2 == 0:
                P.op("act", (lambda e, pa=pa, t0=t0, n=n, off=off: e.copy(out=pre[:, t0 + off:t0 + off + n], in_=pa[:, :n])),
                     reads=[pa_b], writes=[pre_b[bi]])
            else:
                P.op("dve", (lambda e, pa=pa, t0=t0, n=n, off=off: e.tensor_copy(out=pre[:, t0 + off:t0 + off + n], in_=pa[:, :n])),
                     reads=[pa_b], writes=[pre_b[bi]])
        for si, (ts, tn, off) in enumerate(SEG):
            rb = [pre_b[0]] if si == 0 else [pre_b[1:9]]
            for j in range(5):
                src0 = ts + off - 2 + j
                if j == 0:
                    P.op("dve", (lambda e, c=c, ts=ts, tn=tn, src0=src0: e.tensor_scalar(
                        out=cacc[:, ts:ts + tn], in0=pre[:, src0:src0 + tn], scalar1=cw[:, c, 0:1], scalar2=None,
                        op0=ALU.mult)), reads=rb + [cw_b], writes=[cacc_b[si]])
                else:
                    P.op("dve", (lambda e, c=c, j=j, ts=ts, tn=tn, src0=src0: e.scalar_tensor_tensor(
                        out=cacc[:, ts:ts + tn], in0=pre[:, src0:src0 + tn], scalar=cw[:, c, j:j + 1],
                        in1=cacc[:, ts:ts + tn], op0=ALU.mult, op1=ALU.add)),
                        reads=rb + [cw_b, cacc_b[si]], writes=[cacc_b[si]])
        P.op("act", (lambda e, c=c: e.activation(out=sil[:, 0:TC], in_=cacc[:, 0:TC], func=AF.Silu, bias=cbias[:, c:c + 1])),
             reads=[cacc_b[0], cbias_b], writes=[sil_b[0:2]])
        P.op("act", (lambda e, c=c: e.activation(out=sil[:, TC:T], in_=cacc[:, TC:T], func=AF.Silu, bias=cbias[:, c:c + 1])),
             reads=[cacc_b[1], cbias_b], writes=[sil_b[2:NT]])
        self.dma("sp", XBCT[c * 128:(c + 1) * 128, :], sil[:], reads=[sil_b], writes=[XBCT_b[c]])
        if c < 6:
            for i in range(NT):
                pT, pTb = psT[i % 2]
                kt, kt_b = ktm[i % 2]
                P.op("pe", (lambda e, pT=pT, i=i: e.transpose(out=pT[:, 0:128], in_=sil[:, i * 128:(i + 1) * 128],
                                                           identity=ident[:])), reads=[sil_b[i], ident_b], writes=[pTb])
                P.op("dve", (lambda e, pT=pT, kt=kt: e.tensor_copy(out=kt[:], in_=pT[:, 0:128])), reads=[pTb], writes=[kt_b])
                self.dma("sp", XBtm[:, i, c * 128:(c + 1) * 128], kt[:], reads=[kt_b], writes=[XBtm_b[i]])
    vst = [self.sb(st, "ovst%d" % i, [128, 512], BF16) for i in range(2)]
    zst = [self.sb(st, "ozst%d" % i, [128, 512], F32) for i in range(2)]
    dst_ = [self.sb(st, "odst%d" % i, [128, 16], F32) for i in range(2)]
    n_tm = 0
    for i in range(NT):
        lhs = lambda kc, i=i: uT[:, kc, i * 128:(i + 1) * 128]
        for name, c0, cn in [("v", 1024, 512), ("z", 1536, 512), ("dt", 3072, 16)]:
            pt, pt_b = ptm[n_tm % 2]
            n_tm += 1
            for kc in range(8):
                P.op("pe", (lambda e, pt=pt, kc=kc, c0=c0, cn=cn, lhs=lhs: e.matmul(
                    pt[:, :cn], lhsT=lhs(kc), rhs=w[:, kc, c0:c0 + cn], start=(kc == 0), stop=(kc == 7))),
                    reads=[uT_b[i], w_b[kc]], writes=[pt_b])
            if name == "v":
                vt, vt_b = vst[i % 2]
                P.op("act", (lambda e, vt=vt, pt=pt: e.copy(out=vt[:], in_=pt[:])), reads=[pt_b], writes=[vt_b])
                self.dma("sp", VC[:, i, :], vt[:], reads=[vt_b], writes=[VC_b[i]])
            elif name == "z":
                zt, zt_b = zst[i % 2]
                P.op("act", (lambda e, zt=zt, pt=pt: e.activation(out=zt[:], in_=pt[:], func=AF.Silu)), reads=[pt_b], writes=[zt_b])
                self.dma("sp", Z[:, i, :], zt[:], reads=[zt_b], writes=[Z_b[i]])
            else:
                dt_, dt_b = dst_[i % 2]
                P.op("act", (lambda e, dt_=dt_, pt=pt: e.copy(out=dt_[:], in_=pt[:, 0:16])), reads=[pt_b], writes=[dt_b])
                self.dma("sp", DT[:, i, :], dt_[:], reads=[dt_b], writes=[DT_b[i]])


KB.proj_odd = _proj_odd


def _diff_phase(self, l):
    P = self.P
    QCT, QCT_b = self.scr["QCT"]; KCT, KCT_b = self.scr["KCT"]; VC, VC_b = self.scr["VC"]
    CATT, CATT_b = self.scr["CATT"]
    lam_init = 0.8 - 0.6 * math.exp(-0.3 * l)
    NSB = 8
    with ExitStack() as st:
        qT, qT_b = self.sb(st, "dqT", [128, 4, T], BF16, 4)
        kz, kz_b = self.sb(st, "dkz", [128, 4, 2, T], BF16, 4)
        vv, vv_b = self.sb(st, "dvv", [128, NT, 512], BF16)
        cs = self.load_consts(st, ["ones_h", "ones_f"])
        ones_h, onesh_b = cs["ones_h"]; ones_f, onesf_b = cs["ones_f"]
        P.op("pool", lambda e: e.memset(kz[:], 0.0), writes=[kz_b])
        for h in range(4):
            self.dma("sp", qT[:, h, :], QCT[h * 128:(h + 1) * 128, :], reads=[QCT_b], writes=[qT_b[h]])
            for m in range(2):
                self.dma("sp", kz[m * 64:(m + 1) * 64, h, m, :], KCT[h * 128 + m * 64:h * 128 + (m + 1) * 64, :],
                         reads=[KCT_b], writes=[kz_b[h]])
        self.dma("sp", vv[:], VC.ap(), reads=[VC_b], writes=[vv_b])
        lv, lv_b = self.sb(st, "dlv", [128, 4, 64], F32)
        self.dma("sp", lv[:], self.din["c_lambda"].rearrange("(o a) d -> o a d", o=1).broadcast_to([128, 4, 64]), writes=[lv_b])
        lsc, lsc_b = self.sb(st, "dlsc", [128, 8], F32)
        lpr, lpr_b = self.sb(st, "dlpr", [128, 2, 64], F32)
        P.op("dve", lambda e: e.tensor_tensor(out=lpr[:], in0=lv[:, 0:4:2, :], in1=lv[:, 1:4:2, :], op=ALU.mult), reads=[lv_b], writes=[lpr_b])
        P.op("dve", lambda e: e.reduce_sum(out=lsc[:, 0:2], in_=lpr[:], axis=AX.X), reads=[lpr_b], writes=[lsc_b])
        P.op("act", lambda e: e.activation(out=lsc[:, 2:4], in_=lsc[:, 0:2], func=AF.Exp), reads=[lsc_b], writes=[lsc_b])
        P.op("dve", lambda e: e.tensor_tensor(out=lsc[:, 4:5], in0=lsc[:, 3:4], in1=lsc[:, 2:3], op=ALU.subtract), reads=[lsc_b], writes=[lsc_b])
        P.op("dve", lambda e: e.tensor_scalar(out=lsc[:, 5:6], in0=lsc[:, 4:5], scalar1=-lam_init, scalar2=None, op0=ALU.add), reads=[lsc_b], writes=[lsc_b])
        cg, cg_b = self.sb(st, "dcg", [128, 4], F32)
        self.dma_nc("sp", cg[:], self.din["c_norm_g"].rearrange("o (h p) -> p (o h)", p=128), writes=[cg_b])
        P.op("dve", lambda e: e.tensor_scalar(out=cg[:], in0=cg[:], scalar1=1.0 - lam_init, scalar2=None, op0=ALU.mult), reads=[cg_b], writes=[cg_b])
        pS = [self.ps(st, "dpS%d" % i, [128, 512]) for i in range(2)]
        pO = [self.ps(st, "dpO%d" % i, [128, 512]) for i in range(2)]
        pD = [self.ps(st, "dpD%d" % i, [128, 512]) for i in range(2)]
        pN, pN_b = self.ps(st, "dpN", [128, 512])
        ET = [self.sb(st, "dET%d" % i, [128, 512], BF16) for i in range(4)]
        rr = [self.sb(st, "drr%d" % i, [128, 512], F32) for i in range(2)]
        tt = [self.sb(st, "dtt%d" % i, [128, 512], F32) for i in range(2)]
        oo, oo_b = self.sb(st, "doo", [128, 512], F32)
        sq, sq_b = self.sb(st, "dsq", [128, 512], F32)
        rs, rs_b = self.sb(st, "drs", [128, 512], F32)
        ob = [self.sb(st, "dob%d" % i, [128, 512], BF16) for i in range(2)]
        pS = pS + [self.ps(st, "dpS2", [128, 512])]
        LA = 2
        units = [(sbk, h, m, j) for sbk in range(NSB) for h in range(4) for m in range(2) for j in range(NT)]
        NU = len(units)
        ets = {}

        def finish_hm(sbk, h, m):
            po, po_b = pO[m]; pd, pd_b = pD[m]
            r_, r_b = rr[m]; t_, t_b = tt[m]
            q0 = TC + sbk * 512
            P.op("dve", (lambda e, r_=r_, pd=pd: e.reciprocal(out=r_[:], in_=pd[:])), reads=[pd_b], writes=[r_b])
            P.op("dve", (lambda e, t_=t_, po=po, r_=r_: e.tensor_tensor(out=t_[:], in0=po[:], in1=r_[:], op=ALU.mult)),
                 reads=[po_b, r_b], writes=[t_b])
            if m == 1:
                P.op("dve", lambda e: e.scalar_tensor_tensor(out=oo[:], in0=tt[1][0][:], scalar=lsc[:, 5:6], in1=tt[0][0][:],
                                                             op0=ALU.mult, op1=ALU.add), reads=[tt[0][1], tt[1][1], lsc_b], writes=[oo_b])
                P.op("dve", lambda e: e.tensor_tensor(out=sq[:], in0=oo[:], in1=oo[:], op=ALU.mult), reads=[oo_b], writes=[sq_b])
                P.op("pe", lambda e: e.matmul(pN[:], lhsT=ones_f[:], rhs=sq[:], start=True, stop=True), reads=[onesf_b, sq_b], writes=[pN_b])
                P.op("act", lambda e: e.activation(out=rs[:], in_=pN[:], func=AF.Ln, scale=1.0 / 128, bias=EPS), reads=[pN_b], writes=[rs_b])
                P.op("act", lambda e: e.activation(out=rs[:], in_=rs[:], func=AF.Exp, scale=-0.5), reads=[rs_b], writes=[rs_b])
                o_, o_b = ob[h % 2]
                P.op("dve", (lambda e, o_=o_, h=h: e.scalar_tensor_tensor(out=o_[:], in0=oo[:], scalar=cg[:, h:h + 1], in1=rs[:],
                                                                         op0=ALU.mult, op1=ALU.mult)), reads=[oo_b, cg_b, rs_b], writes=[o_b])
                self.dma("sp", CATT[h * 128:(h + 1) * 128, q0:q0 + 512], o_[:], reads=[o_b], writes=[CATT_b[q0 // 128:q0 // 128 + 4]])

        for idx in range(NU + LA):
            if idx < NU:
                sbk, h, m, j = units[idx]
                q0 = TC + sbk * 512
                ps_, ps_b = pS[idx % 3]
                et, et_b = ET[idx % 4]
                ets[idx] = (et, et_b)
                P.op("pe", (lambda e, ps_=ps_, h=h, m=m, j=j, q0=q0: e.matmul(
                    ps_[:], lhsT=kz[:, h, m, j * 128:(j + 1) * 128], rhs=qT[:, h, q0:q0 + 512], start=True, stop=True)),
                    reads=[kz_b[h], qT_b[h]], writes=[ps_b])
                P.op("act", (lambda e, et=et, ps_=ps_: e.activation(out=et[:], in_=ps_[:], func=AF.Exp, scale=0.125)),
                     reads=[ps_b], writes=[et_b])
            k = idx - LA
            if k >= 0:
                sbk, h, m, j = units[k]
                po, po_b = pO[m]; pd, pd_b = pD[m]
                et, et_b = ets.pop(k)
                P.op("pe", (lambda e, po=po, et=et, j=j, h=h: e.matmul(
                    po[:], lhsT=vv[:, j, h * 128:(h + 1) * 128], rhs=et[:], start=(j == 0), stop=(j == NT - 1))),
                    reads=[vv_b, et_b], writes=[po_b])
                P.op("pe", (lambda e, pd=pd, et=et, j=j: e.matmul(
                    pd[:], lhsT=ones_h[:], rhs=et[:], start=(j == 0), stop=(j == NT - 1))),
                    reads=[onesh_b, et_b], writes=[pd_b])
                if j == NT - 1:
                    finish_hm(sbk, h, m)
        self.end_phase()


KB.phase_diff = _diff_phase


def _ssd_phase(self, l):
    P = self.P
    XBCT, XBCT_b = self.scr["XBCT"]; XBtm, XBtm_b = self.scr["XBtm"]; Z, Z_b = self.scr["Z"]; DT, DT_b = self.scr["DT"]
    CATT, CATT_b = self.scr["CATT"]
    C = NT
    with ExitStack() as st:
        bT, bT_b = self.sb(st, "sbT", [128, 2, T], BF16, 2)
        cT, cT_b = self.sb(st, "scT", [128, 2, T], BF16, 2)
        dtr, dtr_b = self.sb(st, "sdtr", [128, C, 16], F32)
        cs = self.load_consts(st, ["tri", "mbias", "ones_f", "ident_f", "ident_h"])
        tri, tri_b = cs["tri"]; mbias, mbias_b = cs["mbias"]; ones_f, ones_b = cs["ones_f"]
        ident_f, identf_b = cs["ident_f"]; ident_h, identh_b = cs["ident_h"]
        for g in range(2):
            self.dma("sp", bT[:, g, :], XBCT[512 + g * 128:512 + (g + 1) * 128, :], reads=[XBCT_b], writes=[bT_b[g]])
            self.dma("sp", cT[:, g, :], XBCT[768 + g * 128:768 + (g + 1) * 128, :], reads=[XBCT_b], writes=[cT_b[g]])
        self.dma("sp", dtr[:], DT.ap(), reads=[DT_b], writes=[dtr_b])
        dtb, dtb_b = self.sb(st, "sdtb", [128, 16], F32)
        negA, negA_b = self.sb(st, "snegA", [128, 16], F32)
        skp, skp_b = self.sb(st, "sskp", [128, 8], F32)
        dng, dng_b = self.sb(st, "sdng", [128, 512], F32)
        self.dma("sp", dtb[:], self.din["d_dt_bias"][0:1, :].broadcast_to([128, 16]), writes=[dtb_b])
        self.dma("sp", negA[:], self.din["d_a_log"][0:1, :].broadcast_to([128, 16]), writes=[negA_b])
        self.dma("sp", skp[:], self.din["d_skip"][0:1, :].broadcast_to([128, 8]), writes=[skp_b])
        self.dma("sp", dng[:], self.din["d_norm_g"][0:1, :].broadcast_to([128, 512]), writes=[dng_b])
        P.op("act", lambda e: e.activation(out=negA[:], in_=negA[:], func=AF.Exp), reads=[negA_b], writes=[negA_b])
        P.op("dve", lambda e: e.tensor_scalar(out=negA[:], in0=negA[:], scalar1=-1.0, scalar2=None, op0=ALU.mult), reads=[negA_b], writes=[negA_b])
        dt_, dt_b = self.sb(st, "sdt", [128, C, 16], F32)
        A, A_b = self.sb(st, "sA", [128, C, 16], F32)
        li, li_b = self.sb(st, "sli", [128, C, 16], F32)
        NB, NB_b = self.sb(st, "sNB", [128, C, 16], F32)
        Wt, Wt_b = self.sb(st, "sWt", [128, C, 16], F32)
        ETOT, ETOT_b = self.sb(st, "sETOT", [128, C, 16], F32)
        pbig = [self.ps(st, "spb%d" % i, [128, 1024]) for i in range(2)]
        pb = [self.ps(st, "sp%d" % i, [128, 512]) for i in range(3)]
        pT, pT_b = self.ps(st, "spT", [128, 1024], BF16)
        P.op("dve", lambda e: e.tensor_tensor(out=dt_[:], in0=dtr[:], in1=dtb[:].rearrange("p (o j) -> p o j", o=1).broadcast_to([128, C, 16]), op=ALU.add),
             reads=[dtr_b, dtb_b], writes=[dt_b])
        P.op("act", lambda e: e.activation(out=dt_[:], in_=dt_[:], func=AF.Exp), reads=[dt_b], writes=[dt_b])
        P.op("act", lambda e: e.activation(out=dt_[:], in_=dt_[:], func=AF.Ln, bias=1.0), reads=[dt_b], writes=[dt_b])
        P.op("act", lambda e: e.activation(out=li[:], in_=dt_[:], func=AF.Ln), reads=[dt_b], writes=[li_b])
        P.op("dve", lambda e: e.tensor_tensor(out=A[:], in0=dt_[:], in1=negA[:].rearrange("p (o j) -> p o j", o=1).broadcast_to([128, C, 16]), op=ALU.mult),
             reads=[dt_b, negA_b], writes=[A_b])
        for d in range(2):
            pcs, pcs_b = pb[0]
            ptot, ptot_b = pb[1]
            P.op("pe", (lambda e, pcs=pcs, d=d: e.matmul(pcs[:, 0:C * 8], lhsT=tri[:, d * 128:(d + 1) * 128],
                                                       rhs=A[:, :, d * 8:(d + 1) * 8], start=True, stop=True)),
                 reads=[tri_b, A_b], writes=[pcs_b])
            P.op("pe", (lambda e, ptot=ptot, d=d: e.matmul(ptot[:, 0:C * 8], lhsT=ones_f[:],
                                                         rhs=A[:, :, d * 8:(d + 1) * 8], start=True, stop=True)),
                 reads=[ones_b, A_b], writes=[ptot_b])
            P.op("dve", (lambda e, pcs=pcs, d=d: e.tensor_tensor(out=NB[:, :, d * 8:(d + 1) * 8], in0=li[:, :, d * 8:(d + 1) * 8],
                                                               in1=pcs[:, 0:C * 8].rearrange("p (c j) -> p c j", j=8), op=ALU.subtract)),
                 reads=[li_b, pcs_b], writes=[NB_b])
            P.op("dve", (lambda e, ptot=ptot, d=d: e.tensor_tensor(out=Wt[:, :, d * 8:(d + 1) * 8], in0=NB[:, :, d * 8:(d + 1) * 8],
                                                                 in1=ptot[:, 0:C * 8].rearrange("p (c j) -> p c j", j=8), op=ALU.add)),
                 reads=[ptot_b, NB_b], writes=[Wt_b])
            P.op("act", (lambda e, ptot=ptot, d=d: e.activation(out=ETOT[:, :, d * 8:(d + 1) * 8],
                                                              in_=ptot[:, 0:C * 8].rearrange("p (c j) -> p c j", j=8), func=AF.Exp)),
                 reads=[ptot_b], writes=[ETOT_b])
        P.op("act", lambda e: e.activation(out=Wt[:], in_=Wt[:], func=AF.Exp), reads=[Wt_b], writes=[Wt_b])
        Sst = [self.sb(st, "sS%d" % d, [128, C, 2, 256], BF16, C) for d in range(2)]
        run = [self.sb(st, "srun%d" % d, [128, 2, 256], F32) for d in range(2)]
        order = [list(range(C)), [1, 0] + list(range(C - 1, 1, -1))]
        xst = [[self.sb(st, "sx%d_%d" % (d, i), [128, 768], BF16) for i in range(2)] for d in range(2)]
        xw = [[self.sb(st, "sxw%d_%d" % (d, i), [128, 512], BF16) for i in range(2)] for d in range(2)]
        for d in range(2):
            P.op("pool", (lambda e, d=d: e.memset(run[d][0][:], 0.0)), writes=[run[d][1]])
            P.op("pool", (lambda e, d=d: e.memset(Sst[d][0][:, order[d][0], :, :], 0.0)), writes=[Sst[d][1][order[d][0]]])
        for k in range(C - 1):
            for d in range(2):
                c = order[d][k]
                xt, xt_b = xst[d][k % 2]
                xwt, xwt_b = xw[d][k % 2]
                rt, rt_b = run[d]
                S_, S_b = Sst[d]
                self.dma("sp", xt[:], XBtm[:, c, :], reads=[XBtm_b], writes=[xt_b])
                P.op("dve", (lambda e, xwt=xwt, xt=xt, c=c, d=d: e.tensor_tensor(
                    out=xwt[:].rearrange("p (h x) -> p h x", h=8), in0=xt[:, 0:512].rearrange("p (h x) -> p h x", h=8),
                    in1=Wt[:, c, d * 8:(d + 1) * 8].rearrange("p (h o) -> p h o", o=1).broadcast_to([128, 8, 64]), op=ALU.mult)),
                    reads=[xt_b, Wt_b], writes=[xwt_b])
                for g in range(2):
                    pst, pst_b = pb[g] if d == 0 else (pb[2] if g == 0 else pbig[0])
                    P.op("pe", (lambda e, pst=pst, xt=xt, xwt=xwt, g=g: e.matmul(
                        pst[:, 0:256], lhsT=xt[:, 512 + g * 128:512 + (g + 1) * 128], rhs=xwt[:, g * 256:(g + 1) * 256],
                        start=True, stop=True)), reads=[xt_b, xwt_b], writes=[pst_b])
                    P.op("dve", (lambda e, rt=rt, g=g, c=c, d=d: e.tensor_tensor(
                        out=rt[:, g, :].rearrange("p (h x) -> p h x", h=4), in0=rt[:, g, :].rearrange("p (h x) -> p h x", h=4),
                        in1=ETOT[:, c, d * 8 + g * 4:d * 8 + g * 4 + 4].rearrange("p (h o) -> p h o", o=1).broadcast_to([128, 4, 64]),
                        op=ALU.mult)), reads=[rt_b, ETOT_b], writes=[rt_b])
                    P.op("dve", (lambda e, rt=rt, pst=pst, g=g: e.tensor_tensor(out=rt[:, g, :], in0=rt[:, g, :], in1=pst[:, 0:256], op=ALU.add)),
                         reads=[rt_b, pst_b], writes=[rt_b])
                cn = order[d][k + 1]
                P.op("act", (lambda e, S_=S_, rt=rt, cn=cn: e.copy(out=S_[:, cn, :, :], in_=rt[:])), reads=[rt_b], writes=[S_b[cn]])
        TA, TA_b = self.sb(st, "sTA", [128, 1024], F32)
        X, X_b = self.sb(st, "sX", [128, 1024], F32)
        Ebc, Ebc_b = self.sb(st, "sEbc", [128, 1024], F32)
        Edec, Edec_b = self.sb(st, "sEdec", [128, 1024], F32)
        MT = [self.sb(st, "sMT%d" % i, [128, 1024], BF16) for i in range(2)]
        CeT = [self.sb(st, "sCeT%d" % i, [128, 1024], BF16) for i in range(2)]
        xo = [self.sb(st, "sxo%d" % i, [128, 768], BF16) for i in range(2)]
        zo = [self.sb(st, "szo%d" % i, [128, 512], F32) for i in range(2)]
        y0, y0_b = self.sb(st, "sy0", [128, 512], F32)
        yy, yy_b = self.sb(st, "syy", [128, 512], F32)
        xs_, xs_b = self.sb(st, "sxs", [128, 512], F32)
        ez, ez_b = self.sb(st, "sez", [128, 512], F32)
        sq, sq_b = self.sb(st, "ssq", [128, 512], F32)
        ss, ss_b = self.sb(st, "sss", [128, 8], F32)
        ybf = [self.sb(st, "sybf%d" % i, [128, 512], BF16) for i in range(2)]
        catst = [self.sb(st, "scat%d" % i, [128, 4, 512], BF16, 4) for i in range(2)]
        pbc, pbc_b = pbig[0]; pdec, pdec_b = pbig[1]
        pCB, pCB_b = pb[0]
        pout = [pb[1], pb[2]]
        trv = tri[:, :].rearrange("p (d t) -> p d t", d=2)
        mbv = mbias[:, :].rearrange("p (d t) -> p d t", d=2)
        lat = list(range(2, C))
        for n, c in enumerate(lat):
            t0 = c * 128
            xo_, xo_b = xo[n % 2]; zo_, zo_b = zo[n % 2]; yb, yb_b = ybf[n % 2]
            self.dma("sp", xo_[:], XBtm[:, c, :], reads=[XBtm_b], writes=[xo_b])
            self.dma("sp", zo_[:], Z[:, c, :], reads=[Z_b], writes=[zo_b])
            for g in range(2):
                P.op("pe", (lambda e, g=g, t0=t0: e.matmul(pCB[:, g * 128:(g + 1) * 128], lhsT=bT[:, g, t0:t0 + 128],
                                                          rhs=cT[:, g, t0:t0 + 128], start=True, stop=True)),
                     reads=[bT_b[g], cT_b[g]], writes=[pCB_b])
            for d in range(2):
                mt, mt_b = MT[d]; ce, ce_b = CeT[d]
                po, po_b = pout[d]
                P.op("dve", (lambda e, c=c, d=d: e.tensor_tensor(
                    out=TA[:].rearrange("p (h t) -> p h t", h=8),
                    in0=trv[:, d:d + 1, :].broadcast_to([128, 8, 128]),
                    in1=A[:, c, d * 8:(d + 1) * 8].rearrange("p (h o) -> p h o", o=1).broadcast_to([128, 8, 128]), op=ALU.mult)),
                    reads=[tri_b, A_b], writes=[TA_b])
                P.op("dve", (lambda e, c=c, d=d: e.tensor_tensor(
                    out=X[:].rearrange("p (h t) -> p h t", h=8),
                    in0=mbv[:, d:d + 1, :].broadcast_to([128, 8, 128]),
                    in1=NB[:, c, d * 8:(d + 1) * 8].rearrange("p (h o) -> p h o", o=1).broadcast_to([128, 8, 128]), op=ALU.add)),
                    reads=[mbias_b, NB_b], writes=[X_b])
                for hf in range(2):
                    P.op("pe", (lambda e, hf=hf: e.matmul(pbc[:, hf * 512:(hf + 1) * 512], lhsT=ones_f[:], rhs=TA[:, hf * 512:(hf + 1) * 512],
                                                        start=True, stop=True)), reads=[ones_b, TA_b], writes=[pbc_b])
                for hf in range(2):
                    P.op("pe", (lambda e, hf=hf: e.matmul(pdec[:, hf * 512:(hf + 1) * 512], lhsT=ones_f[:], rhs=TA[:, hf * 512:(hf + 1) * 512],
                                                        start=True, stop=False)), reads=[ones_b, TA_b], writes=[pdec_b])
                    P.op("pe", (lambda e, hf=hf: e.matmul(pdec[:, hf * 512:(hf + 1) * 512], lhsT=ident_f[:], rhs=X[:, hf * 512:(hf + 1) * 512],
                                                        start=False, stop=True)), reads=[identf_b, X_b], writes=[pdec_b])
                P.op("act", lambda e: e.activation(out=Ebc[:], in_=pbc[:], func=AF.Exp), reads=[pbc_b], writes=[Ebc_b])
                P.op("act", lambda e: e.activation(out=Edec[:], in_=pdec[:], func=AF.Exp), reads=[pdec_b], writes=[Edec_b])
                P.op("dve", (lambda e, mt=mt: e.tensor_tensor(
                    out=mt[:].rearrange("p (g h t) -> p g h t", g=2, h=4), in0=Edec[:].rearrange("p (g h t) -> p g h t", g=2, h=4),
                    in1=pCB[:, 0:256].rearrange("p (g o t) -> p g o t", g=2, o=1).broadcast_to([128, 2, 4, 128]), op=ALU.mult)),
                    reads=[Edec_b, pCB_b], writes=[mt_b])
                P.op("pool", (lambda e, ce=ce, t0=t0: e.tensor_tensor(
                    out=ce[:].rearrange("p (g h t) -> p g h t", g=2, h=4), in0=Ebc[:].rearrange("p (g h t) -> p g h t", g=2, h=4),
                    in1=cT[:, :, t0:t0 + 128].rearrange("p g (o t) -> p g o t", o=1).broadcast_to([128, 2, 4, 128]), op=ALU.mult)),
                    reads=[Ebc_b, cT_b], writes=[ce_b])
                for h8 in range(8):
                    g, hh = h8 // 4, h8 % 4
                    P.op("pe", (lambda e, po=po, mt=mt, xo_=xo_, h8=h8: e.matmul(
                        po[:, h8 * 64:(h8 + 1) * 64], lhsT=mt[:, h8 * 128:(h8 + 1) * 128], rhs=xo_[:, h8 * 64:(h8 + 1) * 64],
                        start=True, stop=False)), reads=[mt_b, xo_b], writes=[po_b])
                    P.op("pe", (lambda e, po=po, ce=ce, h8=h8, g=g, hh=hh, d=d, c=c: e.matmul(
                        po[:, h8 * 64:(h8 + 1) * 64], lhsT=ce[:, h8 * 128:(h8 + 1) * 128], rhs=Sst[d][0][:, c, g, hh * 64:(hh + 1) * 64],
                        start=False, stop=True)), reads=[ce_b, Sst[d][1][c]], writes=[po_b])
            P.op("act", lambda e: e.copy(out=y0[:], in_=pout[0][0][:]), reads=[pout[0][1]], writes=[y0_b])
            P.op("dve", lambda e: e.tensor_tensor(out=yy[:], in0=y0[:], in1=pout[1][0][:], op=ALU.add), reads=[y0_b, pout[1][1]], writes=[yy_b])
            P.op("dve", (lambda e, xo_=xo_: e.tensor_tensor(
                out=xs_[:].rearrange("p (h x) -> p h x", h=8), in0=xo_[:, 0:512].rearrange("p (h x) -> p h x", h=8),
                in1=skp[:].rearrange("p (h o) -> p h o", o=1).broadcast_to([128, 8, 64]), op=ALU.mult)), reads=[xo_b, skp_b], writes=[xs_b])
            P.op("dve", lambda e: e.tensor_tensor(out=yy[:], in0=yy[:], in1=xs_[:], op=ALU.add), reads=[yy_b, xs_b], writes=[yy_b])
            P.op("dve", (lambda e, zo_=zo_: e.tensor_tensor(out=yy[:], in0=yy[:], in1=zo_[:], op=ALU.mult)), reads=[yy_b, zo_b], writes=[yy_b])
            for g in range(2):
                P.op("act", (lambda e, g=g: e.activation(out=sq[:, g * 256:(g + 1) * 256], in_=yy[:, g * 256:(g + 1) * 256], func=AF.Square,
                                                        accum_out=ss[:, g:g + 1])), reads=[yy_b], writes=[sq_b, ss_b])
            P.op("act", lambda e: e.activation(out=ss[:, 2:4], in_=ss[:, 0:2], func=AF.Ln, scale=1.0 / 256, bias=EPS), reads=[ss_b], writes=[ss_b])
            P.op("act", lambda e: e.activation(out=ss[:, 0:2], in_=ss[:, 2:4], func=AF.Exp, scale=-0.5), reads=[ss_b], writes=[ss_b])
            P.op("dve", lambda e: e.tensor_tensor(out=yy[:].rearrange("p (g x) -> p g x", g=2), in0=yy[:].rearrange("p (g x) -> p g x", g=2),
                                                  in1=ss[:, 0:2].rearrange("p (g o) -> p g o", o=1).broadcast_to([128, 2, 256]), op=ALU.mult),
                 reads=[yy_b, ss_b], writes=[yy_b])
            P.op("dve", (lambda e, yb=yb: e.tensor_tensor(out=yb[:], in0=yy[:], in1=dng[:], op=ALU.mult)), reads=[yy_b, dng_b], writes=[yb_b])
            ct_, ct_b = catst[(n // 4) % 2]
            for kc in range(4):
                P.op("pe", (lambda e, yb=yb, kc=kc: e.transpose(out=pT[:, kc * 128:(kc + 1) * 128], in_=yb[:, kc * 128:(kc + 1) * 128],
                                                              identity=ident_h[:])), reads=[yb_b, identh_b], writes=[pT_b])
            P.op("act", (lambda e, ct_=ct_, n=n: e.copy(out=ct_[:, :, (n % 4) * 128:(n % 4 + 1) * 128],
                                                       in_=pT[:, 0:512].rearrange("p (k t) -> p k t", k=4))),
                 reads=[pT_b], writes=[ct_b[n % 4]])
            if n % 4 == 3 or n == len(lat) - 1:
                nt = ((n % 4) + 1) * 128
                tk0 = (lat[n] - (n % 4)) * 128
                self.dma("sp", CATT[512:1024, tk0:tk0 + nt].rearrange("(k p) t -> p k t", p=128), ct_[:, :, 0:nt],
                         reads=[ct_b], writes=[CATT_b[tk0 // 128:(tk0 + nt) // 128]])
        self.end_phase()


KB.phase_ssd = _ssd_phase


def kernel(**inputs):
    inputs = {k: np.asarray(v) for k, v in inputs.items()}
    n = 8
    kb = KB(layers=(0, 1))
    nc = kb.build()
    in_maps = make_in_maps(inputs, list(range(n)))
    res = run_bass_kernel_spmd(nc, in_maps, core_ids=list(range(n)))
    out = np.stack([np.asarray(r["out"], dtype=np.float32) for r in res.results], axis=0)
    return out
```
